# Optimizing a Trainium2 kernel written in Bass

```python
import jax, jax.numpy as jnp
from jax import lax
import numpy as np

D_MODEL = 1024
BATCH = 2
SEQ = 8192
DEPTH = 2

GRID_W = 64
CTX_LEN = 256
N_DIR = 2
LRU_WIDTH = 512
LRU_HEADS = 8
LRU_HEAD_DIM = LRU_WIDTH // LRU_HEADS
LRU_CONV = 4
LRU_C = 8.0
CONV_WIDTH = 512
CONV_GROUPS = 8
CONV_K = 31
CONV_PAD = CONV_K // 2
MIX_WIDTH = LRU_WIDTH + CONV_WIDTH
IN_WIDTH = 2 * LRU_WIDTH + 3 * CONV_WIDTH
EPS = 1e-6

kernel_name = "hybrid_rglru_conformer_prefix_block"


def rms_norm(x, g):
    xf = x.astype(jnp.float32)
    y = xf * lax.rsqrt(jnp.mean(xf * xf, axis=-1, keepdims=True) + EPS)
    return (y * g.astype(jnp.float32)).astype(x.dtype)


def layer_norm(x, g, b):
    xf = x.astype(jnp.float32)
    xc = xf - jnp.mean(xf, axis=-1, keepdims=True)
    var = jnp.mean(xc * xc, axis=-1, keepdims=True)
    return (xc * lax.rsqrt(var + EPS) * g.astype(jnp.float32) + b.astype(jnp.float32)).astype(x.dtype)


def depthwise_conv1d(u, taps, pad):
    return lax.conv_general_dilated(
        u, taps[:, None, :].astype(u.dtype), window_strides=(1,), padding=[pad],
        dimension_numbers=('NWC', 'WIO', 'NWC'), feature_group_count=u.shape[-1])


def rglru_coeffs(v, w_r, b_r, w_i, b_i, lam):
    bn, t, _ = v.shape
    vh = v.reshape(bn, t, LRU_HEADS, LRU_HEAD_DIM)
    r = jax.nn.sigmoid((jnp.einsum('bthi,hij->bthj', vh, w_r).reshape(bn, t, LRU_WIDTH) + b_r).astype(jnp.float32))
    i = jax.nn.sigmoid((jnp.einsum('bthi,hij->bthj', vh, w_i).reshape(bn, t, LRU_WIDTH) + b_i).astype(jnp.float32))
    log_a = -LRU_C * r * jax.nn.softplus(-lam.astype(jnp.float32))
    a = jnp.exp(log_a)
    b = jnp.sqrt(-jnp.expm1(2.0 * log_a)) * (i * v.astype(jnp.float32))
    return a, b


def linear_scan(a, b, h0, reverse):
    if reverse:
        a = jnp.flip(a, axis=1)
        b = jnp.flip(b, axis=1)

    def combine(left, right):
        return (left[0] * right[0], right[0] * left[1] + right[1])

    a_cum, h = lax.associative_scan(combine, (a, b), axis=1)
    h = h + a_cum * h0[:, None, :]
    h_last = h[:, -1]
    if reverse:
        h = jnp.flip(h, axis=1)
    return h, h_last


def rglru_branch(vc_in, vx_in, conv_w, conv_b, w_r, b_r, w_i, b_i, lam):
    out_c = jnp.zeros(vc_in.shape, jnp.float32)
    out_x = jnp.zeros(vx_in.shape, jnp.float32)
    for d in range(N_DIR):
        pad = (LRU_CONV - 1, 0) if d == 0 else (0, LRU_CONV - 1)
        vc = depthwise_conv1d(vc_in, conv_w[d], pad) + conv_b[d]
        ac, bc = rglru_coeffs(vc, w_r[d], b_r[d], w_i[d], b_i[d], lam[d])
        h0 = jnp.zeros((vc.shape[0], LRU_WIDTH), jnp.float32)
        hc, hc_last = linear_scan(ac, bc, h0, reverse=(d == 1))
        vx = depthwise_conv1d(vx_in, conv_w[d], pad) + conv_b[d]
        ax, bx = rglru_coeffs(vx, w_r[d], b_r[d], w_i[d], b_i[d], lam[d])
        hx, _ = linear_scan(ax, bx, hc_last, reverse=(d == 1))
        out_c = out_c + hc
        out_x = out_x + hx
    return out_c, out_x


def conformer_latent(val, glu, dw_w, dw_b, ln_g, ln_b, rows):
    v = val * jax.nn.sigmoid(glu)
    bn = v.shape[0]
    half = CONV_WIDTH // 2
    grid = v.reshape(bn, rows, GRID_W, CONV_WIDTH)
    taps_h = dw_w[:, :half][None, :, None, :].astype(v.dtype)
    taps_v = dw_w[:, half:][:, None, None, :].astype(v.dtype)
    yh = lax.conv_general_dilated(grid[..., :half], taps_h, (1, 1), [(0, 0), (CONV_PAD, CONV_PAD)],
                                  dimension_numbers=('NHWC', 'HWIO', 'NHWC'), feature_group_count=half)
    yv = lax.conv_general_dilated(grid[..., half:], taps_v, (1, 1), [(CONV_PAD, CONV_PAD), (0, 0)],
                                  dimension_numbers=('NHWC', 'HWIO', 'NHWC'), feature_group_count=half)
    y = jnp.concatenate([yh, yv], axis=-1).reshape(bn, rows * GRID_W, CONV_WIDTH) + dw_b
    return jax.nn.silu(layer_norm(y, ln_g, ln_b))


def conformer_context(val, glu, dw_w, dw_b, ln_g, ln_b):
    v = val * jax.nn.sigmoid(glu)
    y = depthwise_conv1d(v, dw_w, (CONV_PAD, CONV_PAD)) + dw_b
    return jax.nn.silu(layer_norm(y, ln_g, ln_b))


def hybrid_layer(x, xc, c_act, cctx_act, w_mod, b_mod, g_pre, g_post, w_in, conv_a_w, conv_a_b,
                 w_rgate, b_rgate, w_igate, b_igate, lru_lambda, dw_w, dw_b, ln_g, ln_b, w_out,
                 rows, update_ctx):
    mod_x = (c_act @ w_mod + b_mod)[:, None, :]
    mod_c = (cctx_act @ w_mod + b_mod)[None, None, :]
    shift_x, scale_x, gate_x = jnp.split(mod_x, 3, axis=-1)
    shift_c, scale_c, gate_c = jnp.split(mod_c, 3, axis=-1)

    hx = rms_norm(x, g_pre) * (1.0 + scale_x) + shift_x
    hc = rms_norm(xc, g_pre) * (1.0 + scale_c) + shift_c

    splits = [LRU_WIDTH, 2 * LRU_WIDTH, 2 * LRU_WIDTH + CONV_WIDTH, 2 * LRU_WIDTH + 2 * CONV_WIDTH]
    ux = hx @ w_in
    a_val_x, a_gate_x, b_val_x, b_glu_x, b_gate_x = jnp.split(ux, splits, axis=-1)
    if update_ctx:
        uc = hc @ w_in
        a_val_c, a_gate_c, b_val_c, b_glu_c, b_gate_c = jnp.split(uc, splits, axis=-1)
    else:
        a_val_c = hc @ w_in[:, :LRU_WIDTH]

    rec_c, rec_x = rglru_branch(a_val_c, a_val_x, conv_a_w, conv_a_b, w_rgate, b_rgate,
                                w_igate, b_igate, lru_lambda)
    conv_x = conformer_latent(b_val_x, b_glu_x, dw_w, dw_b, ln_g, ln_b, rows)

    mix_x = jnp.concatenate([rec_x.astype(x.dtype) * jax.nn.silu(a_gate_x),
                             conv_x * jax.nn.silu(b_gate_x)], axis=-1) @ w_out
    x = x + gate_x * rms_norm(mix_x, g_post)

    if update_ctx:
        conv_c = conformer_context(b_val_c, b_glu_c, dw_w, dw_b, ln_g, ln_b)
        mix_c = jnp.concatenate([rec_c.astype(xc.dtype) * jax.nn.silu(a_gate_c),
                                 conv_c * jax.nn.silu(b_gate_c)], axis=-1) @ w_out
        xc = xc + gate_c * rms_norm(mix_c, g_post)
    return x, xc


def setup_inputs(seed: int = 0) -> dict:
    key = jax.random.key(seed)
    ks = jax.random.split(key, 24)
    D = D_MODEL
    f32 = jnp.float32
    nrm = lambda k, shape, s: jax.random.normal(k, shape, f32) * s
    x = nrm(ks[0], (BATCH, SEQ, D), 1.0)
    c = nrm(ks[1], (BATCH, D), 1.0)
    ctx = nrm(ks[2], (BATCH, CTX_LEN, D), 1.0)
    c_ctx = nrm(ks[3], (D,), 1.0)
    w_mod = nrm(ks[4], (DEPTH, D, 3 * D), 0.5 * D ** -0.5)
    b_mod = nrm(ks[5], (DEPTH, 3 * D), 0.02)
    g_pre = 1.0 + nrm(ks[6], (DEPTH, D), 0.02)
    g_post = 1.0 + nrm(ks[7], (DEPTH, D), 0.02)
    w_in = nrm(ks[8], (DEPTH, D, IN_WIDTH), D ** -0.5)
    conv_a_w = nrm(ks[9], (DEPTH, N_DIR, LRU_CONV, LRU_WIDTH), LRU_CONV ** -0.5)
    conv_a_b = nrm(ks[10], (DEPTH, N_DIR, LRU_WIDTH), 0.02)
    w_rgate = nrm(ks[11], (DEPTH, N_DIR, LRU_HEADS, LRU_HEAD_DIM, LRU_HEAD_DIM), LRU_HEAD_DIM ** -0.5)
    b_rgate = nrm(ks[12], (DEPTH, N_DIR, LRU_WIDTH), 0.02)
    w_igate = nrm(ks[13], (DEPTH, N_DIR, LRU_HEADS, LRU_HEAD_DIM, LRU_HEAD_DIM), LRU_HEAD_DIM ** -0.5)
    b_igate = nrm(ks[14], (DEPTH, N_DIR, LRU_WIDTH), 0.02)
    u = jax.random.uniform(ks[15], (DEPTH, N_DIR, LRU_WIDTH), f32, 0.9, 0.999)
    a0 = u ** (1.0 / LRU_C)
    lru_lambda = jnp.log(a0) - jnp.log1p(-a0)
    dw_w = nrm(ks[16], (DEPTH, CONV_K, CONV_WIDTH), CONV_K ** -0.5)
    dw_b = nrm(ks[17], (DEPTH, CONV_WIDTH), 0.02)
    ln_g = 1.0 + nrm(ks[18], (DEPTH, CONV_WIDTH), 0.02)
    ln_b = nrm(ks[19], (DEPTH, CONV_WIDTH), 0.02)
    w_out = nrm(ks[20], (DEPTH, MIX_WIDTH, D), MIX_WIDTH ** -0.5)
    return {"x": x, "c": c, "ctx": ctx, "c_ctx": c_ctx, "w_mod": w_mod, "b_mod": b_mod,
            "g_pre": g_pre, "g_post": g_post, "w_in": w_in, "conv_a_w": conv_a_w,
            "conv_a_b": conv_a_b, "w_rgate": w_rgate, "b_rgate": b_rgate, "w_igate": w_igate,
            "b_igate": b_igate, "lru_lambda": lru_lambda, "dw_w": dw_w, "dw_b": dw_b,
            "ln_g": ln_g, "ln_b": ln_b, "w_out": w_out}


def reference(x, c, ctx, c_ctx, w_mod, b_mod, g_pre, g_post, w_in, conv_a_w, conv_a_b,
              w_rgate, b_rgate, w_igate, b_igate, lru_lambda, dw_w, dw_b, ln_g, ln_b, w_out):
    rows = x.shape[1] // GRID_W
    c_act = jax.nn.silu(c)
    cctx_act = jax.nn.silu(c_ctx)
    xc = ctx
    for l in range(DEPTH):
        x, xc = hybrid_layer(x, xc, c_act, cctx_act, w_mod[l], b_mod[l], g_pre[l], g_post[l], w_in[l],
                             conv_a_w[l], conv_a_b[l], w_rgate[l], b_rgate[l], w_igate[l], b_igate[l],
                             lru_lambda[l], dw_w[l], dw_b[l], ln_g[l], ln_b[l], w_out[l],
                             rows, update_ctx=(l < DEPTH - 1))
    return x
```

```python
import contextlib
import numpy as np
import concourse.bass as bass
import concourse.mybir as mybir
from concourse.bass_utils import run_bass_kernel_spmd

F32 = mybir.dt.float32
BF16 = mybir.dt.bfloat16
AF = mybir.ActivationFunctionType
ALU = mybir.AluOpType

D = 1024
T = 8192
TC = 256
L = 2
NCH = T // 512
EPS = 1e-6
NP = 224
C_GPRE, C_BSH, C_BSC, C_BGT, C_GPOST = 0, 8, 16, 224, 232
C_CAW, C_CAB, C_BR, C_BI, C_LAM = 24, 56, 64, 72, 80
C_DWW, C_DWB, C_LNG, C_LNB = 88, 212, 216, 220
NPP = 240

COMPUTE = ("pe", "act", "dve", "pool")
DMAQ = ("sp",)
NDSEM = 16
SAME_ENGINE_SYNC = True
DEBUG_OUT = False


class Prog:
    def __init__(self, nc):
        self.nc = nc
        self.engs = COMPUTE + DMAQ
        self.ops = {e: [] for e in self.engs}
        self.count = {e: 0 for e in self.engs}
        self.last_write = {}
        self.readers = {}
        self.seen = {e: {e2: 0 for e2 in COMPUTE} for e in self.engs}
        self.seen_dma = {e: set() for e in self.engs}
        self.force_sync = False
        self.sync_ops = set()

    def op(self, eng, fn, reads=(), writes=(), sync=False):
        idx = self.count[eng] + 1
        self.count[eng] = idx
        sync = sync or self.force_sync
        if sync:
            self.sync_ops.add((eng, idx))
        deps = set()
        for b in reads:
            if b in self.last_write:
                deps.add(self.last_write[b])
        for b in writes:
            if b in self.last_write:
                deps.add(self.last_write[b])
            deps.update(self.readers.get(b, ()))
        waits = {}
        dwaits = set()
        for (e2, i2) in deps:
            if e2 == eng and (eng == "pe" or not (SAME_ENGINE_SYNC or sync or (e2, i2) in self.sync_ops)):
                continue
            if e2 in DMAQ:
                if (e2, i2) in self.seen_dma[eng]:
                    continue
                dwaits.add((e2, i2))
            else:
                if self.seen[eng][e2] >= i2:
                    continue
                waits[e2] = max(waits.get(e2, 0), i2)
        for e2, i2 in waits.items():
            self.seen[eng][e2] = i2
        for d in dwaits:
            self.seen_dma[eng].add(d)
        self.ops[eng].append((fn, waits, dwaits, idx))
        me = (eng, idx)
        for b in writes:
            self.last_write[b] = me
            self.readers[b] = []
        for b in reads:
            if b not in writes:
                self.readers.setdefault(b, []).append(me)
        return me

    def emit(self):
        nc = self.nc
        with contextlib.ExitStack() as st:
            sems = {e: st.enter_context(nc.semaphore("sem_" + e)) for e in COMPUTE}
            dsems = {e: [st.enter_context(nc.semaphore(f"dsem_{e}_{i}")) for i in range(NDSEM)] for e in DMAQ}
            block = st.enter_context(nc.Block())
            handles = {"pe": "tensor", "act": "scalar", "dve": "vector", "pool": "gpsimd", "sp": "sync"}

            def make(eng):
                def body(e):
                    for (fn, waits, dwaits, idx) in self.ops[eng]:
                        for e2, i2 in waits.items():
                            e.wait_ge(sems[e2], i2)
                        for (e2, i2) in dwaits:
                            k = i2 - 1
                            e.wait_ge(dsems[e2][k % NDSEM], 16 * (k // NDSEM + 1))
                        if eng in DMAQ and idx - 1 >= NDSEM:
                            k = idx - 1
                            e.wait_ge(dsems[eng][k % NDSEM], 16 * (k // NDSEM))
                        ins = fn(e)
                        if eng in DMAQ:
                            k = idx - 1
                            ins.then_inc(dsems[eng][k % NDSEM], 16)
                        else:
                            ins.then_inc(sems[eng], 1)
                    if eng in DMAQ:
                        n = self.count[eng]
                        for s in range(NDSEM):
                            cnt = len(range(s, n, NDSEM))
                            if cnt:
                                e.wait_ge(dsems[eng][s], 16 * cnt)
                return body

            for eng in self.engs:
                getattr(block, handles[eng])(make(eng))


def build_program():
    nc = bass.Bass("TRN2", target_bir_lowering=False)

    def dram(name, shape, dt=F32, kind="ExternalInput"):
        return nc.dram_tensor(name, shape, dt, kind=kind).ap()

    x_d = dram("x", [T, D])
    ctx_d = dram("ctx", [TC, D])
    cc_d = dram("cc", [128, 16])
    pp_d = dram("pp", [L, 128, NPP])
    wmod_d = dram("w_mod", [L, D, 3 * D])
    win_d = dram("w_in", [L, D, 2560])
    wout_d = dram("w_out", [L, D, D])
    wg_d = dram("wg", [L, 128, 16 * 128])
    ident_d = dram("ident", [128, 128])
    out_d = dram("out", [T, D], kind="ExternalOutput")
    skind = "ExternalOutput" if DEBUG_OUT else "Internal"
    x1_d = dram("x1", [T, D], kind=skind)
    xc1_d = dram("xc1", [TC, D], kind=skind)
    recf_d = dram("recf", [512, T], kind="Internal")
    y1_d = dram("y1", [256, T], kind="Internal")
    v2_d = dram("v2", [256, T], BF16, kind="Internal")
    g_d = dram("gsc", [L, 2, D], kind="Internal")
    hx_d = dram("hx", [D, T], BF16, kind="Internal")
    hxc_d = dram("hxc", [D, TC], BF16, kind="Internal")
    av_d = dram("avs", [512, T], kind="Internal")
    avc_d = dram("avcs", [512, TC], kind="Internal")

    with contextlib.ExitStack() as st:
        def sb(name, shape, dt=F32):
            return st.enter_context(nc.sbuf_tensor("s_" + name, shape, dt))

        ps = [st.enter_context(nc.psum_tensor(f"ps{i}", [128, 512], F32)) for i in range(8)]
        ps_ctr = [0]

        def nb():
            i = ps_ctr[0] % 8
            ps_ctr[0] += 1
            return ps[i], f"ps{i}"

        P = Prog(nc)

        ident = sb("ident", [128, 128])
        identb = sb("identb", [128, 128], BF16)
        onesm = sb("onesm", [128, 128])
        cc = sb("cc", [128, 16])
        PP = sb("PP", [128, NPP])
        modT = sb("modT", [128, 48])
        Apre = sb("Apre", [128, 16])
        Bpre = sb("Bpre", [128, 16])
        gT = sb("gT", [128, 16])
        gTt = sb("gTt", [16, 128])
        G = sb("G", [128, D])
        PPH = sb("PPH", [128, 140])
        kaph = sb("kaph", [128, 8])
        cq = sb("cq", [128, 1])
        cpow = sb("cpow", [128, 2])
        kap = sb("kap", [128, 8])
        carry = sb("carry", [128, 8])
        big = sb("big", [128, 4096])
        w_in_bf = sb("w_in_bf", [128, 8, 12 * 128], BF16)
        w_out_bf = sb("w_out_bf", [128, 8, D], BF16)
        wg_bf = sb("wg_bf", [128, 16, 128], BF16)
        dgm = sb("dgm", [128, 31, 2, 128], BF16)
        xs = [sb("xs0", [128, 4, D])]
        hxTb = [sb("hxT", [128, 8, 512], BF16), sb("hxT2", [128, 8, 512], BF16)]
        dummy = sb("dummy", [128, 2])

        ss = [sb(f"ss{i}", [128, 4]) for i in range(2)]
        rstd = [sb(f"rstd{i}", [128, 4]) for i in range(2)]
        Dgall = sb("Dgall", [128, 2, 4, 128])
        Dg = [Dgall[:, i] for i in range(2)]
        av = sb("av", [128, 4, 518])
        NSET = 2
        lt = [{k: sb(f"l{k}{i}", [128, 512]) for k in ("A", "B", "C", "H")} for i in range(NSET)]
        lvc = [sb(f"lvc{i}", [128, 512]) for i in range(4)]
        lvb = [sb(f"lvb{i}", [128, 512], BF16) for i in range(4)]
        sg = sb("sg", [128, 512])
        sag = sg
        vtmp = [sb("vtmp0", [128, 512], BF16)] * 2
        V1pad = sb("V1pad", [128, 2, 8, 94], BF16)
        V2win = sb("V2win", [128, 2, 38 * 64], BF16)
        Vc = sb("Vc", [128, 4, TC + 30], BF16)

        ctx_y = sb("ctx_y", [128, 2, TC])
        ctx_recf = sb("ctx_recf", [128, 4, TC])
        ysq = sb("ysq", [128, 512])
        mean = sb("mean", [128, 512])
        rs_ln = sb("rs_ln", [128, 512])
        zt = sb("zt", [128, 512])
        sbg = sb("sbg", [128, 512])
        ysqB = sb("ysqB", [128, 512])
        y1buf = [ysq, mean]
        mixA = [sb(f"mixA{i}", [128, 4, 512], BF16) for i in range(2)]
        mixA.append(Dgall[:].rearrange("p a b c -> p (a b c)").bitcast(BF16).rearrange("p (k n) -> p k n", k=4))
        mixB = [sb(f"mixB{i}", [128, 4, 512], BF16) for i in range(2)]
        junk = mixB[0][:].rearrange("p a b -> p (a b)")[:, 0:D]
        JUNK = ["mixB0_0", "mixB0_1"]
        DGN = [f"Dg{pb}_{t}" for pb in range(2) for t in range(4)]
        MA2 = [f"mixA2_{c4}" for c4 in range(4)]
        ptmp = [zt, sbg]
        ss2 = sb("ss2", [128, 2])
        rstd2 = sb("rstd2", [128, 1])
        recf = big[:, 0:2048].rearrange("p (c t) -> p c t", c=4)
        ybuf = big[:, 2048:4096].rearrange("p (c t) -> p c t", c=4)
        stage = big[:].rearrange("p (k n) -> p k n", k=8)

        def ACT(out, in_, func, r, w, sync=False, **kw):
            P.op("act", lambda e: e.activation(out=out, in_=in_, func=func, **kw), r, w, sync=sync)

        def MM(out, lhsT, rhs, start, stop, r, w):
            P.op("pe", lambda e: e.matmul(out, lhsT=lhsT, rhs=rhs, start=start, stop=stop), r, w)

        def TS(eng, out, in0, s1, s2, op0, op1, r, w, sync=False):
            if s2 is None:
                P.op(eng, lambda e: e.tensor_scalar(out=out, in0=in0, scalar1=s1, scalar2=None, op0=op0), r, w, sync=sync)
            else:
                P.op(eng, lambda e: e.tensor_scalar(out=out, in0=in0, scalar1=s1, scalar2=s2, op0=op0, op1=op1), r, w, sync=sync)

        def STT(eng, out, in0, scalar, in1, op0, op1, r, w):
            P.op(eng, lambda e: e.scalar_tensor_tensor(out=out, in0=in0, scalar=scalar, in1=in1, op0=op0, op1=op1), r, w)

        def TT(eng, out, in0, in1, op, r, w, sync=False):
            P.op(eng, lambda e: e.tensor_tensor(out=out, in0=in0, in1=in1, op=op), r, w, sync=sync)

        def CP(eng, out, in_, r, w, sync=False):
            P.op(eng, lambda e: e.tensor_copy(out=out, in_=in_), r, w, sync=sync)

        def MS(eng, out, val, w, sync=False):
            P.op(eng, lambda e: e.memset(out, val), (), w, sync=sync)

        def POW(out, in_, which, n, r, w, sync=False):
            ex = cpow[:, which:which + 1].to_broadcast([128, n])
            P.op("pool", lambda e: e.tensor_tensor(out=out, in0=in_, in1=ex, op=ALU.pow), list(r) + ["cpow"], w, sync=sync)

        def DMA(out, in_, r, w):
            P.op("sp", lambda e: e.dma_start(out=out, in_=in_), r, w)

        P.force_sync = True
        DMA(ident[:], ident_d, [], ["ident"])
        DMA(cc[:], cc_d, [], ["cc"])
        CP("pool", identb[:], ident[:], ["ident"], ["identb"])
        MS("pool", onesm[:], 1.0 / 512.0, ["onesm"])
        MS("pool", cq[:], 0.25, ["cq"])
        MS("pool", cpow[:, 0:1], 0.5, ["cpow"])
        MS("pool", cpow[:, 1:2], -0.5, ["cpow"])
        MS("pool", V1pad[:], 0.0, ["V1pad0", "V1pad1"])
        MS("pool", Vc[:], 0.0, ["Vc"])
        ACT(cc[:], cc[:], AF.Silu, ["cc"], ["cc"])
        P.force_sync = False

        BIG = ["big.lo", "big.hi"]

        def load_w_in(l, ocs, groups=(0, 1, 2)):
            for g in groups:
                oc0 = ocs[4 * g]
                src = win_d[l].rearrange("(k p) n -> p k n", p=128)[:, :, oc0 * 128:(oc0 + 4) * 128]
                DMA(stage, src, [], BIG)
                for kc in range(8):
                    if kc % 3 == 2:
                        ACT(w_in_bf[:, kc, g * 512:(g + 1) * 512], stage[:, kc, :], AF.Copy, BIG, ["w_in_bf"])
                    else:
                        CP("dve", w_in_bf[:, kc, g * 512:(g + 1) * 512], stage[:, kc, :], BIG, ["w_in_bf"])

        def load_w_out(l):
            for g in range(2):
                src = wout_d[l].rearrange("(k p) n -> p k n", p=128)[:, :, g * 512:(g + 1) * 512]
                DMA(stage, src, [], BIG)
                for kc in range(8):
                    if kc % 3 == 2:
                        ACT(w_out_bf[:, kc, g * 512:(g + 1) * 512], stage[:, kc, :], AF.Copy, BIG, ["w_out_bf"])
                    else:
                        CP("dve", w_out_bf[:, kc, g * 512:(g + 1) * 512], stage[:, kc, :], BIG, ["w_out_bf"])

        def build_dgm(ccs):
            for i, c4 in enumerate(ccs):
                in0 = identb[:].unsqueeze(1).to_broadcast([128, 31, 128])
                in1 = PPH[:, c4:124:4].unsqueeze(2).to_broadcast([128, 31, 128])
                TT("dve", dgm[:, :, i, :], in0, in1, ALU.mult, ["identb", "PPH"], ["dgm"])

        def layer_setup(l):
            P.force_sync = True
            layer_setup_(l)
            P.force_sync = False

        def layer_setup_(l):
            DMA(PP[:], pp_d[l], [], ["PP"])
            pt, pn = nb()
            for hs in range(12):
                half = big[:, (hs % 2) * 2048:(hs % 2 + 1) * 2048].rearrange("p (k n) -> p k n", k=8)
                hname = BIG[hs % 2]
                src = wmod_d[l].rearrange("(k p) n -> p k n", p=128)[:, :, hs * 256:(hs + 1) * 256]
                P.op("sp", lambda e, half=half, src=src: e.dma_start(out=half, in_=src), [], [hname])
                for o in range(2):
                    oc = hs * 2 + o
                    for kc in range(8):
                        MM(pt[:, oc * 2:oc * 2 + 2], half[:, kc, o * 128:(o + 1) * 128], cc[:, kc * 2:kc * 2 + 2],
                           kc == 0, kc == 7, [hname, "cc"], [pn])
            CP("dve", modT[:], pt[:, 0:48], [pn], ["modT"])
            m3 = modT[:].rearrange("p (o j) -> p o j", j=2)
            for j in range(2):
                a_j = Apre[:].rearrange("p (k j) -> p k j", j=2)[:, :, j]
                b_j = Bpre[:].rearrange("p (k j) -> p k j", j=2)[:, :, j]
                TT("dve", a_j, m3[:, 8:16, j], PP[:, C_BSC:C_BSC + 8], ALU.add, ["modT", "PP"], ["Apre"])
                TS("dve", a_j, a_j, 1.0, None, ALU.add, None, ["Apre"], ["Apre"])
                TT("dve", a_j, a_j, PP[:, C_GPRE:C_GPRE + 8], ALU.mult, ["Apre", "PP"], ["Apre"])
                TT("dve", b_j, m3[:, 0:8, j], PP[:, C_BSH:C_BSH + 8], ALU.add, ["modT", "PP"], ["Bpre"])
                g_j = gT[:, j * 8:(j + 1) * 8]
                TT("dve", g_j, m3[:, 16:24, j], PP[:, C_BGT:C_BGT + 8], ALU.add, ["modT", "PP"], ["gT"])
                TT("dve", g_j, g_j, PP[:, C_GPOST:C_GPOST + 8], ALU.mult, ["gT", "PP"], ["gT"])
            pt, pn = nb()
            MM(pt[0:16, 0:128], gT[:, 0:16], ident[:], True, True, ["gT", "ident"], [pn])
            CP("dve", gTt[:], pt[0:16, 0:128], [pn], ["gTt"])
            DMA(g_d[l].rearrange("j (k p) -> (j k) p", p=128), gTt[:], ["gTt"], ["g_d"])
            DMA(big[:, 0:2048], wg_d[l], [], ["big.lo"])
            CP("dve", wg_bf[:].rearrange("p a b -> p (a b)"), big[:, 0:2048], ["big.lo"], ["wg_bf"])
            ACT(kap[:], PP[:, C_LAM:C_LAM + 8], AF.Exp, ["PP"], ["kap"], scale=-1.0)
            ACT(kap[:], kap[:], AF.Ln, ["kap"], ["kap"], bias=1.0)
            TS("dve", kaph[:], kap[:], -4.0, None, ALU.mult, None, ["kap"], ["kaph"])
            TS("dve", kap[:], kap[:], -8.0, None, ALU.mult, None, ["kap"], ["kap"])
            TS("dve", PPH[:, 0:124], PP[:, C_DWW:C_DWW + 124], 0.5, None, ALU.mult, None, ["PP"], ["PPH"])
            TS("dve", PPH[:, 124:140], PP[:, C_BR:C_BR + 16], 0.5, None, ALU.mult, None, ["PP"], ["PPH"])
            MS("pool", carry[:], 0.0, [f"carry{i}" for i in range(8)])

        src_rows_names = [[]]

        def load_x(src_rows, N, slot):
            nt = N // 128
            P.op("sp", lambda e: e.dma_start(out=xs[slot][:, 0:nt, :], in_=src_rows.rearrange("(t p) d -> p t d", p=128)),
                 src_rows_names[0], [f"xs{slot}"])

        def stats(N, slot, pb):
            nt = N // 128
            X = xs[slot]
            MS("dve", ss[pb][:], 0.0, [f"ss{pb}"], sync=True)
            for t in range(nt):
                ACT(junk, X[:, t, :], AF.Square, [f"xs{slot}"], JUNK + [f"ss{pb}"], accum_out=ss[pb][:, t:t + 1])
            TS("dve", rstd[pb][:, 0:nt], ss[pb][:, 0:nt], 1.0 / D, EPS, ALU.mult, ALU.add, [f"ss{pb}"], [f"rstd{pb}"], sync=True)
            POW(rstd[pb][:, 0:nt], rstd[pb][:, 0:nt], 1, nt, [f"rstd{pb}"], [f"rstd{pb}"], sync=True)
            for t in range(nt):
                TS("dve", Dg[pb][:, t, :], ident[:], rstd[pb][:, t:t + 1], None, ALU.mult, None,
                   ["ident", f"rstd{pb}"], [f"Dg{pb}_{t}"], sync=True)

        def transposes(N, j, slot, pb, store=None):
            nt = N // 128
            X = xs[slot]
            hxT = hxTb[0]
            for kc in range(8):
                pt, pn = nb()
                for t in range(nt):
                    MM(pt[:, t * 128:(t + 1) * 128], X[:, t, kc * 128:(kc + 1) * 128], Dg[pb][:, t, :], True, True,
                       [f"xs{slot}", f"Dg{pb}_{t}"], [pn])
                ACT(hxT[:, kc, 0:N], pt[:, 0:N], AF.Identity, [pn, "Apre", "Bpre"], [f"hxT0_{kc}"],
                    scale=Apre[:, kc * 2 + j:kc * 2 + j + 1], bias=Bpre[:, kc * 2 + j:kc * 2 + j + 1])
            if store is not None:
                dst, dname = store
                DMA(dst, hxT[:, :, 0:N], [f"hxT0_{kc}" for kc in range(8)], [dname])

        def inproj(widx, N, hb=0):
            pt, pn = nb()
            for kc in range(8):
                MM(pt[:, 0:N], w_in_bf[:, kc, widx * 128:(widx + 1) * 128], hxTb[hb][:, kc, 0:N], kc == 0, kc == 7,
                   ["w_in_bf", f"hxT{hb}_{kc}"], [pn])
            return pt, pn

        AVN = [f"av{c}" for c in range(4)]

        def a_branch_val(N):
            for c4 in range(4):
                pt, pn = inproj(c4, N)
                ACT(av[:, c4, 3:3 + N], pt[:, 0:N], AF.Identity, [pn], [f"av{c4}"])

        def lruA_all(d, N):
            base = 0 if d == 0 else 3
            wcol = lambda k, c4: PP[:, C_CAW + d * 16 + k * 4 + c4:C_CAW + d * 16 + k * 4 + c4 + 1]
            for c4 in range(4):
                bcol = PP[:, C_CAB + d * 4 + c4:C_CAB + d * 4 + c4 + 1]
                TS("pool", lvc[c4][:, 0:N], av[:, c4, base:base + N], wcol(0, c4), bcol, ALU.mult, ALU.add,
                   [f"av{c4}", "PP"], [f"lvc{c4}"])
            for c4 in range(4):
                vc = lvc[c4]
                for k in range(1, 4):
                    STT("dve", vc[:, 0:N], av[:, c4, base + k:base + k + N], wcol(k, c4), vc[:, 0:N], ALU.mult, ALU.add,
                        [f"av{c4}", "PP", f"lvc{c4}"], [f"lvc{c4}"])
            for c4 in range(4):
                ACT(lvb[c4][:, 0:N], lvc[c4][:, 0:N], AF.Copy, [f"lvc{c4}"], [f"lvb{c4}"])

        def lruA_one(d, c4, N):
            base = 0 if d == 0 else 3
            wcol = lambda k: PP[:, C_CAW + d * 16 + k * 4 + c4:C_CAW + d * 16 + k * 4 + c4 + 1]
            bcol = PP[:, C_CAB + d * 4 + c4:C_CAB + d * 4 + c4 + 1]
            vc = lvc[c4]
            TS("pool", vc[:, 0:N], av[:, c4, base:base + N], wcol(0), bcol, ALU.mult, ALU.add, [f"av{c4}", "PP"], [f"lvc{c4}"])
            for k in range(1, 4):
                STT("dve", vc[:, 0:N], av[:, c4, base + k:base + k + N], wcol(k), vc[:, 0:N], ALU.mult, ALU.add,
                    [f"av{c4}", "PP", f"lvc{c4}"], [f"lvc{c4}"])
            ACT(lvb[c4][:, 0:N], vc[:, 0:N], AF.Copy, [f"lvc{c4}"], [f"lvb{c4}"])

        def lruB1(d, c4, N):
            si = c4 % NSET
            S = lt[si]
            nm = lambda k: f"l{k}{si}"
            A, B, C = S["A"], S["B"], S["C"]
            pr, prn = nb()
            MM(pr[:, 0:N], wg_bf[:, (d * 2 + 0) * 4 + c4, :], lvb[c4][:, 0:N], True, True, ["wg_bf", f"lvb{c4}"], [prn])
            pi, pin = nb()
            MM(pi[:, 0:N], wg_bf[:, (d * 2 + 1) * 4 + c4, :], lvb[c4][:, 0:N], True, True, ["wg_bf", f"lvb{c4}"], [pin])
            kc_ = d * 4 + c4
            hb_r = PPH[:, 124 + kc_:125 + kc_]
            hb_i = PPH[:, 132 + kc_:133 + kc_]
            ACT(A[:, 0:N], pr[:, 0:N], AF.Tanh, [prn, "PPH"], [nm("A")], scale=0.5, bias=hb_r)
            ACT(B[:, 0:N], pi[:, 0:N], AF.Tanh, [pin, "PPH"], [nm("B")], scale=0.5, bias=hb_i)
            ACT(C[:, 0:N], A[:, 0:N], AF.Exp, [nm("A"), "kap"], [nm("C")], scale=kap[:, kc_:kc_ + 1], bias=kap[:, kc_:kc_ + 1])
            ACT(A[:, 0:N], A[:, 0:N], AF.Exp, [nm("A"), "kaph"], [nm("A")], scale=kaph[:, kc_:kc_ + 1], bias=kaph[:, kc_:kc_ + 1])

        def lruBs(d, c4, N):
            si = c4 % NSET
            C = lt[si]["C"]
            ACT(C[:, 0:N], C[:, 0:N], AF.Sqrt, [f"lC{si}", "cq"], [f"lC{si}"], scale=-0.25, bias=cq[:, 0:1])

        def lruB2(d, c4, N):
            si = c4 % NSET
            S = lt[si]
            nm = lambda k: f"l{k}{si}"
            vc, A, B, C, H = lvc[c4], S["A"], S["B"], S["C"], S["H"]
            kc_ = d * 4 + c4
            STT("dve", B[:, 0:N], B[:, 0:N], 1.0, vc[:, 0:N], ALU.add, ALU.mult, [nm("B"), f"lvc{c4}"], [nm("B")])
            TT("dve", B[:, 0:N], B[:, 0:N], C[:, 0:N], ALU.mult, [nm("B"), nm("C")], [nm("B")])
            cin = carry[:, kc_:kc_ + 1]
            if d == 0:
                P.op("dve", lambda e: e.tensor_tensor_scan(out=H[:, 0:N], data0=A[:, 0:N], data1=B[:, 0:N], initial=cin,
                                                           op0=ALU.mult, op1=ALU.add),
                     [nm("A"), nm("B"), f"carry{kc_}"], [nm("H")])
                CP("pool", cin, H[:, N - 1:N], [nm("H")], [f"carry{kc_}"], sync=True)
            else:
                P.op("dve", lambda e: e.tensor_tensor_scan(out=H[:, 0:N][:, ::-1], data0=A[:, 0:N][:, ::-1],
                                                           data1=B[:, 0:N][:, ::-1], initial=cin,
                                                           op0=ALU.mult, op1=ALU.add),
                     [nm("A"), nm("B"), f"carry{kc_}"], [nm("H")], sync=True)
                CP("pool", cin, H[:, 0:1], [nm("H")], [f"carry{kc_}"], sync=True)
            return H, nm("H")

        def lru_all(d, N, consume, extra_sqrt=lambda: None, inter=lambda k: None, hook=lambda tag: None):
            for half in range(2):
                c4s = (2 * half, 2 * half + 1)
                for c4 in c4s:
                    lruB1(d, c4, N)
                hook(f"b1_{half}")
                for c4 in c4s:
                    lruBs(d, c4, N)
                if half == 0:
                    extra_sqrt()
                for c4 in c4s:
                    inter(c4)
                    H, hn = lruB2(d, c4, N)
                    consume(c4, H, hn)
                hook(f"done_{half}")

        vt_ctr = [0]

        def pass1_chunk(l, c, nxt, snx):
            isctx = c < 0
            N = TC if isctx else 512
            full = (not isctx) or (l == 0)
            a_branch_val(N)
            if isctx:
                DMA(avc_d.rearrange("(c p) t -> p c t", p=128), av[:, :, 3:3 + N], AVN, ["avd-1"])
            else:
                DMA(av_d.rearrange("(c p) t -> p c t", p=128)[:, :, c * 512:(c + 1) * 512], av[:, :, 3:3 + N], AVN, [f"avd{c}"])
            lruA_all(0, N)
            snx()
            if full:
                for c4 in range(4):
                    pv, pvn = inproj(4 + c4, N)
                    pg, pgn = inproj(8 + c4, N)
                    ACT(sg[:, 0:N], pg[:, 0:N], AF.Tanh, [pgn], ["sg"], scale=0.5)
                    if isctx:
                        STT("dve", Vc[:, c4, 15:15 + N], sg[:, 0:N], 1.0, pv[:, 0:N], ALU.add, ALU.mult, [pvn, "sg"], ["Vc"])
                    elif c4 < 2:
                        STT("dve", V1pad[:, c4, :, 15:79], sg[:, 0:N].rearrange("p (r w) -> p r w", w=64), 1.0,
                            pv[:, 0:N].rearrange("p (r w) -> p r w", w=64), ALU.add, ALU.mult, [pvn, "sg"], [f"V1pad{c4}"])
                    else:
                        vi = vt_ctr[0] % 2
                        vt_ctr[0] += 1
                        STT("dve", vtmp[vi][:], sg[:, 0:N], 1.0, pv[:, 0:N], ALU.add, ALU.mult, [pvn, "sg"], ["vtmp0"])
                        DMA(v2_d[(c4 - 2) * 128:(c4 - 1) * 128, c * 512:(c + 1) * 512], vtmp[vi][:], ["vtmp0"], [f"v2d{c}_{c4}"])
                for c4 in range(2):
                    pt, pn = nb()
                    for k in range(31):
                        if isctx:
                            MM(pt[:, 0:N], dgm[:, k, c4, :], Vc[:, c4, k:k + N], k == 0, k == 30, ["dgm", "Vc"], [pn])
                        else:
                            MM(pt[:].rearrange("p (r w) -> p r w", w=64), dgm[:, k, c4, :], V1pad[:, c4, :, k:k + 64],
                               k == 0, k == 30, ["dgm", f"V1pad{c4}"], [pn])
                    bcol = PP[:, C_DWB + c4:C_DWB + c4 + 1]
                    if isctx:
                        ACT(ctx_y[:, c4, :], pt[:, 0:N], AF.Identity, [pn, "PP"], ["ctx_y"], bias=bcol)
                    else:
                        ACT(y1buf[c4][:], pt[:], AF.Identity, [pn, "PP"], [("ysq", "mean")[c4]], bias=bcol)
                        DMA(y1_d[c4 * 128:(c4 + 1) * 128, c * 512:(c + 1) * 512], y1buf[c4][:], [("ysq", "mean")[c4]], [f"y1d{c}_{c4}"])
            def consume1(c4, H, hn):
                if isctx:
                    if l == 0:
                        CP("pool", ctx_recf[:, c4, :], H[:, 0:N], [hn], ["ctx_recf"])
                else:
                    DMA(recf_d[c4 * 128:(c4 + 1) * 128, c * 512:(c + 1) * 512], H[:, 0:N], [hn], [f"recfd{c}_{c4}"])

            lru_all(0, N, consume1)
            CP("pool", av[:, :, 0:3], av[:, :, N:N + 3], AVN, AVN, sync=True)
            nxt()

        def p2_conv(i, N, isctx):
            c4 = 2 + i
            pt, pn = nb()
            for k in range(31):
                if isctx:
                    MM(pt[:, 0:N], dgm[:, k, i, :], Vc[:, c4, k:k + N], k == 0, k == 30, ["dgm", "Vc"], [pn])
                else:
                    MM(pt[:, 0:N], dgm[:, k, i, :], V2win[:, i, k * 64:k * 64 + 512], k == 0, k == 30,
                       ["dgm", "V2win"], [pn])
            ACT(ybuf[:, c4, 0:N], pt[:, 0:N], AF.Identity, [pn, "PP"], ["big.hi"], bias=PP[:, C_DWB + c4:C_DWB + c4 + 1])

        def p2_stats(N, isctx):
            if isctx:
                CP("pool", ybuf[:, 0:2, 0:N], ctx_y[:], ["ctx_y"], ["big.hi"])
            pm, pmn = nb()
            for c4 in range(4):
                MM(pm[:, 0:N], onesm[:], ybuf[:, c4, 0:N], c4 == 0, c4 == 3, ["onesm", "big.hi"], [pmn])
            pq, pqn = nb()
            for c4 in range(4):
                yq, yqn = ((ysq, "ysq"), (ysqB, "ysqB"))[c4 % 2]
                ACT(yq[:, 0:N], ybuf[:, c4, 0:N], AF.Square, ["big.hi"], [yqn])
                MM(pq[:, 0:N], onesm[:], yq[:, 0:N], c4 == 0, c4 == 3, ["onesm", yqn], [pqn])
            CP("dve", mean[:, 0:N], pm[:, 0:N], [pmn], ["mean"])
            TT("dve", rs_ln[:, 0:N], mean[:, 0:N], mean[:, 0:N], ALU.mult, ["mean"], ["rs_ln"])
            TT("dve", rs_ln[:, 0:N], pq[:, 0:N], rs_ln[:, 0:N], ALU.subtract, [pqn, "rs_ln"], ["rs_ln"])
            TS("dve", rs_ln[:, 0:N], rs_ln[:, 0:N], EPS, None, ALU.add, None, ["rs_ln"], ["rs_ln"])

        def p2_ln_sqrt(N):
            ACT(rs_ln[:, 0:N], rs_ln[:, 0:N], AF.Sqrt, ["rs_ln"], ["rs_ln"])
            P.op("dve", lambda e: e.reciprocal(out=rs_ln[:, 0:N], in_=rs_ln[:, 0:N]), ["rs_ln"], ["rs_ln"])

        def p2_ln_iter(c4, N, hb, mb):
            b = c4 % 2
            p_, pn1 = ((ysq, "ysq"), (ysqB, "ysqB"))[b]
            TT("pool", zt[:, 0:N], ybuf[:, c4, 0:N], mean[:, 0:N], ALU.subtract, ["big.hi", "mean"], ["zt"])
            TT("dve", zt[:, 0:N], zt[:, 0:N], rs_ln[:, 0:N], ALU.mult, ["zt", "rs_ln"], ["zt"])
            ACT(zt[:, 0:N], zt[:, 0:N], AF.Identity, ["zt", "PP"], ["zt"],
                scale=PP[:, C_LNG + c4:C_LNG + c4 + 1], bias=PP[:, C_LNB + c4:C_LNB + c4 + 1])
            ACT(p_[:, 0:N], zt[:, 0:N], AF.Tanh, ["zt"], [pn1], scale=0.5)
            pt, pn = inproj(8 + c4, N, hb)
            ACT(sbg[:, 0:N], pt[:, 0:N], AF.Tanh, [pn], ["sbg"], scale=0.5)
            STT("dve", p_[:, 0:N], p_[:, 0:N], 1.0, zt[:, 0:N], ALU.add, ALU.mult, [pn1, "zt"], [pn1])
            STT("dve", sbg[:, 0:N], sbg[:, 0:N], 1.0, pt[:, 0:N], ALU.add, ALU.mult, [pn, "sbg"], ["sbg"])
            STT("dve", mixB[mb][:, c4, 0:N], p_[:, 0:N], 0.25, sbg[:, 0:N], ALU.mult, ALU.mult, [pn1, "sbg"], [f"mixB{mb}_{c4}"])

        def p2_lru(l, cn, k, extra, inter, hook=lambda tag: None):
            isctx = cn < 0
            N = TC if isctx else 512
            full = (not isctx) or (l == 0)
            hb = k % 2
            par = k % 3

            def consume2(c4, H, hn):
                if not full:
                    return
                pt, pn = inproj(4 + c4, N, hb)
                ACT(sag[:, 0:N], pt[:, 0:N], AF.Tanh, [pn], ["sg"], scale=0.5)
                STT("dve", sag[:, 0:N], sag[:, 0:N], 1.0, pt[:, 0:N], ALU.add, ALU.mult, [pn, "sg"], ["sg"])
                rf = ctx_recf[:, c4, :] if isctx else recf[:, c4, :]
                rfn = "ctx_recf" if isctx else "big.lo"
                TT("pool", H[:, 0:N], H[:, 0:N], rf, ALU.add, [hn, rfn], [hn])
                STT("dve", mixA[par][:, c4, 0:N], H[:, 0:N], 0.5, sag[:, 0:N], ALU.mult, ALU.mult, [hn, "sg"], [f"mixA{par}_{c4}"])

            lru_all(1, N, consume2, extra, inter, hook)

        def p2_out_tile(k, t, dst_rows, dst_name):
            X = xs[0]
            pa, pb_ = k % 3, k % 2
            halves = []
            MS("dve", ss2[:], 0.0, ["ss2"], sync=True)
            for h in range(2):
                pt, pn = nb()
                for kc in range(8):
                    lhs = mixA[pa][:, kc, t * 128:(t + 1) * 128] if kc < 4 else mixB[pb_][:, kc - 4, t * 128:(t + 1) * 128]
                    lname = f"mixA{pa}_{kc}" if kc < 4 else f"mixB{pb_}_{kc - 4}"
                    MM(pt[:], lhs, w_out_bf[:, kc, h * 512:(h + 1) * 512], kc == 0, kc == 7, [lname, "w_out_bf"], [pn])
                ACT(vtmp[0][:], pt[:], AF.Square, [pn], ["vtmp0", "ss2"], accum_out=ss2[:, h:h + 1])
                halves.append((pt, pn))
            TT("dve", rstd2[:], ss2[:, 0:1], ss2[:, 1:2], ALU.add, ["ss2"], ["rstd2"], sync=True)
            TS("dve", rstd2[:], rstd2[:], 1.0 / D, EPS, ALU.mult, ALU.add, ["rstd2"], ["rstd2"], sync=True)
            POW(rstd2[:], rstd2[:], 1, 1, ["rstd2"], ["rstd2"], sync=True)
            for h, (pt, pn) in enumerate(halves):
                STT("dve", ptmp[h][:], pt[:], rstd2[:, 0:1], G[:, h * 512:(h + 1) * 512], ALU.mult, ALU.mult,
                    [pn, "rstd2", "G"], [("zt", "sbg")[h]])
                TT("pool", X[:, t, h * 512:(h + 1) * 512], ptmp[h][:], X[:, t, h * 512:(h + 1) * 512], ALU.add,
                   [("zt", "sbg")[h], f"xr{t}"], [f"xr{t}"])
            deferred_stores.append(lambda: DMA(dst_rows[t * 128:(t + 1) * 128, :], X[:, t, :], [f"xr{t}"], [f"{dst_name}_{t}"]))

        deferred_stores = []

        XR = [f"xr{t}" for t in range(4)]
        for l in range(L):
            xin = x_d if l == 0 else x1_d
            xout = x1_d if l == 0 else out_d
            cin_d = ctx_d if l == 0 else xc1_d
            xname = (lambda c: []) if l == 0 else (lambda c: [f"x1_{c}_{t}" for t in range(4)])
            cname = [] if l == 0 else ["xc1_0", "xc1_1"]
            oname = (lambda c: f"x1_{c}") if l == 0 else (lambda c: f"out_{c}")

            def rows(c):
                return cin_d if c < 0 else xin[c * 512:(c + 1) * 512, :]

            def rnames(c):
                return cname if c < 0 else xname(c)

            def ld(c):
                src_rows_names[0] = rnames(c)
                load_x(rows(c), TC if c < 0 else 512, 0)

            def NJ(c):
                return (TC, 1) if c < 0 else (512, 0)

            def hx_store(c):
                if c < 0:
                    return (hxc_d.rearrange("(k p) t -> p k t", p=128), "hxd-1")
                return (hx_d.rearrange("(k p) t -> p k t", p=128)[:, :, c * 512:(c + 1) * 512], f"hxd{c}")

            layer_setup(l)
            load_w_in(l, [0, 1, 2, 3, 8, 9, 10, 11, 12, 13, 14, 15])
            build_dgm([0, 1])
            MS("pool", av[:], 0.0, AVN)
            P.op("pool", lambda e: e.memset(dummy[:, 0:1], 0.0), XR + MA2, ["xs0"] + DGN)
            seq = [-1] + list(range(NCH))
            ld(seq[0])
            n0, j0 = NJ(seq[0])
            stats(n0, 0, 0)
            transposes(n0, j0, 0, 0, hx_store(seq[0]))
            for i, c in enumerate(seq):
                if i + 1 < len(seq):
                    c1 = seq[i + 1]
                    n1, j1 = NJ(c1)
                    p1 = (i + 1) % 2
                    ld(c1)
                    snx = (lambda n1=n1, p1=p1: stats(n1, 0, p1))
                    nxt = (lambda n1=n1, j1=j1, p1=p1, c1=c1: transposes(n1, j1, 0, p1, hx_store(c1)))
                else:
                    snx = (lambda: None)
                    nxt = (lambda: None)
                pass1_chunk(l, c, nxt, snx)
                if c < 0:
                    MS("pool", av[:], 0.0, AVN)
                if i == 5:
                    load_w_out(l)
            load_w_in(l, [0, 1, 2, 3, 4, 5, 6, 7, 16, 17, 18, 19], groups=(1, 2))
            build_dgm([2, 3])
            MS("pool", av[:], 0.0, AVN)
            P.op("pool", lambda e: e.memset(dummy[:, 1:2], 0.0), ["xs0"] + DGN, XR + MA2)
            if l == 0:
                DMA(G[:], g_d[l, 1].partition_broadcast(128), ["g_d"], ["G"])
            else:
                DMA(G[:], g_d[l, 0].partition_broadcast(128), ["g_d"], ["G"])

            def loads_hx(c, hb):
                n = TC if c < 0 else 512
                if (c >= 0) or (l == 0):
                    src = hxc_d.rearrange("(k p) t -> p k t", p=128) if c < 0 else \
                        hx_d.rearrange("(k p) t -> p k t", p=128)[:, :, c * 512:(c + 1) * 512]
                    DMA(hxTb[hb][:, :, 0:n], src, [f"hxd{c}"], [f"hxT{hb}_{kc}" for kc in range(8)])

            def loads_A(c, hb, with_hx=True):
                n = TC if c < 0 else 512
                if c >= 0:
                    DMA(recf, recf_d.rearrange("(c p) t -> p c t", p=128)[:, :, c * 512:(c + 1) * 512], [f"recfd{c}_{q}" for q in range(4)], ["big.lo"])
                if with_hx:
                    loads_hx(c, hb)
                if c < 0:
                    DMA(av[:, :, 3:3 + n], avc_d.rearrange("(c p) t -> p c t", p=128), ["avd-1"], AVN)
                elif c == NCH - 1:
                    MS("pool", av[:, :, 3 + n:6 + n], 0.0, AVN)
                    DMA(av[:, :, 3:3 + n], av_d.rearrange("(c p) t -> p c t", p=128)[:, :, c * 512:(c + 1) * 512], [f"avd{c}"], AVN)
                else:
                    DMA(av[:, :, 3:6 + n], av_d.rearrange("(c p) t -> p c t", p=128)[:, :, c * 512:(c + 1) * 512 + 3],
                        [f"avd{c}", f"avd{c + 1}"], AVN)

            def loads_B(c):
                DMA(ybuf[:, 0:2, :], y1_d.rearrange("(c p) t -> p c t", p=128)[:, :, c * 512:(c + 1) * 512], [f"y1d{c}_{q}" for q in range(2)], ["big.hi"])
                r0 = 8 * c - 15
                lo, hi = max(0, r0), min(128, r0 + 38)
                if lo > r0 or hi < r0 + 38:
                    MS("pool", V2win[:], 0.0, ["V2win"])
                DMA(V2win[:, :, (lo - r0) * 64:(hi - r0) * 64],
                    v2_d.rearrange("(c p) t -> p c t", p=128)[:, :, lo * 64:hi * 64],
                    [f"v2d{q}_{r}" for q in range(max(0, c - 2), min(NCH, c + 3)) for r in (2, 3)], ["V2win"])

            seq = [-1] + list(range(NCH - 1, -1, -1))
            nseq = len(seq)
            isfull = lambda c: (c >= 0) or (l == 0)
            loads_A(seq[0], 0)
            lruA_all(1, TC)
            p2_lru(l, seq[0], 0, lambda: None, lambda k: None)
            loads_A(seq[1], 1)
            lruA_all(1, 512)
            for i in range(nseq + 1):
                c = seq[i] if i < nseq else None
                cn = seq[i + 1] if i + 1 < nseq else None
                cnn = seq[i + 2] if i + 2 < nseq else None
                cp = seq[i - 1] if i >= 1 else None
                doB = c is not None and isfull(c)
                doO = cp is not None and isfull(cp)
                Nc = TC if (c is not None and c < 0) else 512
                Np = TC if (cp is not None and cp < 0) else 512
                hb = i % 2
                if doO:
                    for t in range(Np // 128):
                        DMA(xs[0][:, t, :], rows(cp)[t * 128:(t + 1) * 128, :], rnames(cp), [f"xr{t}"])
                    orow = xc1_d if cp < 0 else xout[cp * 512:(cp + 1) * 512, :]
                    onm = "xc1" if cp < 0 else oname(cp)

                def hook(tag, c=c, cp=cp, doB=doB, doO=doO, Nc=Nc, Np=Np, i=i):
                    order = ["b1_0", "done_0", "b1_1", "done_1"]
                    t = order.index(tag)
                    if doB and t < 2:
                        p2_conv(t, Nc, c < 0)
                    if doO and t < Np // 128:
                        p2_out_tile(i - 1, t, orow, onm)
                    if doB and t == 2:
                        p2_stats(Nc, c < 0)

                if cn is not None:
                    p2_lru(l, cn, i + 1, lambda: None, lambda k: None, hook)
                else:
                    for tag in ("b1_0", "done_0", "b1_1", "done_1"):
                        hook(tag)
                if doO and cp < 0:
                    DMA(G[:], g_d[l, 0].partition_broadcast(128), ["g_d"], ["G"])
                if cnn is not None:
                    loads_A(cnn, (i + 2) % 2, with_hx=False)
                if doB:
                    p2_ln_sqrt(Nc)
                for k in range(4):
                    if doB:
                        p2_ln_iter(k, Nc, hb, i % 2)
                    if cnn is not None:
                        lruA_one(1, k, 512)
                if cnn is not None:
                    loads_hx(cnn, (i + 2) % 2)
                if cn is not None and cn >= 0:
                    loads_B(cn)
                for st_ in deferred_stores:
                    st_()
                deferred_stores.clear()

        P.emit()
    return nc


def _pack_inputs(inp, b):
    f = lambda a: np.ascontiguousarray(np.asarray(a, dtype=np.float32))
    fm = lambda v, n: v.reshape(n, 128).T
    cc = np.zeros((128, 16), np.float32)
    cc[:, 0::2] = fm(f(inp["c"])[b], 8)
    cc[:, 1::2] = fm(f(inp["c_ctx"]), 8)
    pp = np.zeros((L, 128, NPP), np.float32)
    wg = np.zeros((L, 128, 16, 128), np.float32)
    for l in range(L):
        bm = f(inp["b_mod"])[l]
        pp[l, :, C_GPRE:C_GPRE + 8] = fm(f(inp["g_pre"])[l], 8)
        pp[l, :, C_BSH:C_BSH + 8] = fm(bm[0:D], 8)
        pp[l, :, C_BSC:C_BSC + 8] = fm(bm[D:2 * D], 8)
        pp[l, :, C_BGT:C_BGT + 8] = fm(bm[2 * D:3 * D], 8)
        pp[l, :, C_GPOST:C_GPOST + 8] = fm(f(inp["g_post"])[l], 8)
        for d in range(2):
            for k in range(4):
                pp[l, :, C_CAW + d * 16 + k * 4:C_CAW + d * 16 + k * 4 + 4] = fm(f(inp["conv_a_w"])[l, d, k], 4)
            pp[l, :, C_CAB + d * 4:C_CAB + d * 4 + 4] = fm(f(inp["conv_a_b"])[l, d], 4)
            pp[l, :, C_BR + d * 4:C_BR + d * 4 + 4] = fm(f(inp["b_rgate"])[l, d], 4)
            pp[l, :, C_BI + d * 4:C_BI + d * 4 + 4] = fm(f(inp["b_igate"])[l, d], 4)
            pp[l, :, C_LAM + d * 4:C_LAM + d * 4 + 4] = fm(f(inp["lru_lambda"])[l, d], 4)
            for gi, key in enumerate(("w_rgate", "w_igate")):
                w = f(inp[key])[l, d]
                for c4 in range(4):
                    for hh in range(2):
                        wg[l, hh * 64:(hh + 1) * 64, (d * 2 + gi) * 4 + c4, hh * 64:(hh + 1) * 64] = w[c4 * 2 + hh]
        for k in range(31):
            pp[l, :, C_DWW + k * 4:C_DWW + k * 4 + 4] = fm(f(inp["dw_w"])[l, k], 4)
        pp[l, :, C_DWB:C_DWB + 4] = fm(f(inp["dw_b"])[l], 4)
        pp[l, :, C_LNG:C_LNG + 4] = fm(f(inp["ln_g"])[l], 4)
        pp[l, :, C_LNB:C_LNB + 4] = fm(f(inp["ln_b"])[l], 4)
    return {
        "x": f(inp["x"])[b], "ctx": f(inp["ctx"])[b], "cc": cc, "pp": pp,
        "w_mod": f(inp["w_mod"]), "w_in": f(inp["w_in"]), "w_out": f(inp["w_out"]),
        "wg": wg.reshape(L, 128, 16 * 128), "ident": np.eye(128, dtype=np.float32),
    }


_NC_CACHE = {}


def kernel(**inputs):
    if "nc" not in _NC_CACHE:
        _NC_CACHE["nc"] = build_program()
    nc = _NC_CACHE["nc"]
    maps = [_pack_inputs(inputs, b) for b in range(2)]
    in_maps = [maps[r % 2] for r in range(8)]
    res = run_bass_kernel_spmd(nc, in_maps, core_ids=list(range(8)))
    out = np.stack([np.asarray(res.results[b]["out"], dtype=np.float32) for b in range(2)], axis=0)
    if DEBUG_OUT:
        kernel.debug = [{k: np.asarray(res.results[b][k]) for k in ("x1", "xc1")} for b in range(2)]
    return out
```

```python
import contextlib
import numpy as np
import concourse.bass as bass
import concourse.mybir as mybir
from concourse.bass_utils import run_bass_kernel_spmd

F32 = mybir.dt.float32
BF16 = mybir.dt.bfloat16
AF = mybir.ActivationFunctionType
ALU = mybir.AluOpType

D = 1024
T = 8192
TC = 256
L = 2
NCH = T // 512
EPS = 1e-6
NP = 224
C_GPRE, C_BSH, C_BSC, C_BGT, C_GPOST = 0, 8, 16, 224, 232
C_CAW, C_CAB, C_BR, C_BI, C_LAM = 24, 56, 64, 72, 80
C_DWW, C_DWB, C_LNG, C_LNB = 88, 212, 216, 220
NPP = 240

COMPUTE = ("pe", "act", "dve", "pool")
DMAQ = ("sp",)
NDSEM = 16
SAME_ENGINE_SYNC = True
DEBUG_OUT = False


class Prog:
    def __init__(self, nc):
        self.nc = nc
        self.engs = COMPUTE + DMAQ
        self.ops = {e: [] for e in self.engs}
        self.count = {e: 0 for e in self.engs}
        self.last_write = {}
        self.readers = {}
        self.seen = {e: {e2: 0 for e2 in COMPUTE} for e in self.engs}
        self.seen_dma = {e: set() for e in self.engs}
        self.force_sync = False
        self.sync_ops = set()

    def op(self, eng, fn, reads=(), writes=(), sync=False):
        idx = self.count[eng] + 1
        self.count[eng] = idx
        sync = sync or self.force_sync
        if sync:
            self.sync_ops.add((eng, idx))
        deps = set()
        for b in reads:
            if b in self.last_write:
                deps.add(self.last_write[b])
        for b in writes:
            if b in self.last_write:
                deps.add(self.last_write[b])
            deps.update(self.readers.get(b, ()))
        waits = {}
        dwaits = set()
        for (e2, i2) in deps:
            if e2 == eng and (eng == "pe" or not (SAME_ENGINE_SYNC or sync or (e2, i2) in self.sync_ops)):
                continue
            if e2 in DMAQ:
                if (e2, i2) in self.seen_dma[eng]:
                    continue
                dwaits.add((e2, i2))
            else:
                if self.seen[eng][e2] >= i2:
                    continue
                waits[e2] = max(waits.get(e2, 0), i2)
        for e2, i2 in waits.items():
            self.seen[eng][e2] = i2
        for d in dwaits:
            self.seen_dma[eng].add(d)
        self.ops[eng].append((fn, waits, dwaits, idx))
        me = (eng, idx)
        for b in writes:
            self.last_write[b] = me
            self.readers[b] = []
        for b in reads:
            if b not in writes:
                self.readers.setdefault(b, []).append(me)
        return me

    def emit(self):
        nc = self.nc
        with contextlib.ExitStack() as st:
            sems = {e: st.enter_context(nc.semaphore("sem_" + e)) for e in COMPUTE}
            dsems = {e: [st.enter_context(nc.semaphore(f"dsem_{e}_{i}")) for i in range(NDSEM)] for e in DMAQ}
            block = st.enter_context(nc.Block())
            handles = {"pe": "tensor", "act": "scalar", "dve": "vector", "pool": "gpsimd", "sp": "sync"}

            def make(eng):
                def body(e):
                    for (fn, waits, dwaits, idx) in self.ops[eng]:
                        for e2, i2 in waits.items():
                            e.wait_ge(sems[e2], i2)
                        for (e2, i2) in dwaits:
                            k = i2 - 1
                            e.wait_ge(dsems[e2][k % NDSEM], 16 * (k // NDSEM + 1))
                        if eng in DMAQ and idx - 1 >= NDSEM:
                            k = idx - 1
                            e.wait_ge(dsems[eng][k % NDSEM], 16 * (k // NDSEM))
                        ins = fn(e)
                        if eng in DMAQ:
                            k = idx - 1
                            ins.then_inc(dsems[eng][k % NDSEM], 16)
                        else:
                            ins.then_inc(sems[eng], 1)
                    if eng in DMAQ:
                        n = self.count[eng]
                        for s in range(NDSEM):
                            cnt = len(range(s, n, NDSEM))
                            if cnt:
                                e.wait_ge(dsems[eng][s], 16 * cnt)
                return body

            for eng in self.engs:
                getattr(block, handles[eng])(make(eng))


def build_program():
    nc = bass.Bass("TRN2", target_bir_lowering=False)

    def dram(name, shape, dt=F32, kind="ExternalInput"):
        return nc.dram_tensor(name, shape, dt, kind=kind).ap()

    x_d = dram("x", [T, D])
    ctx_d = dram("ctx", [TC, D])
    cc_d = dram("cc", [128, 16])
    pp_d = dram("pp", [L, 128, NPP])
    wmod_d = dram("w_mod", [L, D, 3 * D])
    win_d = dram("w_in", [L, D, 2560])
    wout_d = dram("w_out", [L, D, D])
    wg_d = dram("wg", [L, 128, 16 * 128])
    ident_d = dram("ident", [128, 128])
    out_d = dram("out", [T, D], kind="ExternalOutput")
    skind = "ExternalOutput" if DEBUG_OUT else "Internal"
    x1_d = dram("x1", [T, D], kind=skind)
    xc1_d = dram("xc1", [TC, D], kind=skind)
    recf_d = dram("recf", [512, T], kind="Internal")
    y1_d = dram("y1", [256, T], kind="Internal")
    v2_d = dram("v2", [256, T], BF16, kind="Internal")
    g_d = dram("gsc", [L, 2, D], kind="Internal")
    hx_d = dram("hx", [D, T], BF16, kind="Internal")
    hxc_d = dram("hxc", [D, TC], BF16, kind="Internal")
    av_d = dram("avs", [512, T], kind="Internal")
    avc_d = dram("avcs", [512, TC], kind="Internal")

    with contextlib.ExitStack() as st:
        def sb(name, shape, dt=F32):
            return st.enter_context(nc.sbuf_tensor("s_" + name, shape, dt))

        ps = [st.enter_context(nc.psum_tensor(f"ps{i}", [128, 512], F32)) for i in range(8)]
        ps_ctr = [0]

        def nb():
            i = ps_ctr[0] % 8
            ps_ctr[0] += 1
            return ps[i], f"ps{i}"

        P = Prog(nc)

        ident = sb("ident", [128, 128])
        identb = sb("identb", [128, 128], BF16)
        onesm = sb("onesm", [128, 128])
        cc = sb("cc", [128, 16])
        PP = sb("PP", [128, NPP])
        modT = sb("modT", [128, 48])
        Apre = sb("Apre", [128, 16])
        Bpre = sb("Bpre", [128, 16])
        gT = sb("gT", [128, 16])
        gTt = sb("gTt", [16, 128])
        G = sb("G", [128, D])
        PPH = sb("PPH", [128, 140])
        kaph = sb("kaph", [128, 8])
        cq = sb("cq", [128, 1])
        cpow = sb("cpow", [128, 2])
        kap = sb("kap", [128, 8])
        carry = sb("carry", [128, 8])
        big = sb("big", [128, 4096])
        w_in_bf = sb("w_in_bf", [128, 8, 12 * 128], BF16)
        w_out_bf = sb("w_out_bf", [128, 8, D], BF16)
        wg_bf = sb("wg_bf", [128, 16, 128], BF16)
        dgm = sb("dgm", [128, 31, 2, 128], BF16)
        xs = [sb("xs0", [128, 4, D])]
        hxTb = [sb("hxT", [128, 8, 512], BF16), sb("hxT2", [128, 8, 512], BF16)]
        dummy = sb("dummy", [128, 2])

        ss = [sb(f"ss{i}", [128, 4]) for i in range(2)]
        rstd = [sb(f"rstd{i}", [128, 4]) for i in range(2)]
        Dgall = sb("Dgall", [128, 2, 4, 128])
        Dg = [Dgall[:, i] for i in range(2)]
        av = sb("av", [128, 4, 518])
        NSET = 2
        lt = [{k: sb(f"l{k}{i}", [128, 512]) for k in ("A", "B", "C", "H")} for i in range(NSET)]
        lvc = [sb(f"lvc{i}", [128, 512]) for i in range(4)]
        lvb = [sb(f"lvb{i}", [128, 512], BF16) for i in range(4)]
        sg = sb("sg", [128, 512])
        sag = sg
        vtmp = [sb("vtmp0", [128, 512], BF16)] * 2
        V1pad = sb("V1pad", [128, 2, 8, 94], BF16)
        V2win = sb("V2win", [128, 2, 38 * 64], BF16)
        Vc = sb("Vc", [128, 4, TC + 30], BF16)

        ctx_y = sb("ctx_y", [128, 2, TC])
        ctx_recf = sb("ctx_recf", [128, 4, TC])
        ysq = sb("ysq", [128, 512])
        mean = sb("mean", [128, 512])
        rs_ln = sb("rs_ln", [128, 512])
        zt = sb("zt", [128, 512])
        sbg = sb("sbg", [128, 512])
        ysqB = sb("ysqB", [128, 512])
        y1buf = [ysq, mean]
        mixA = [sb(f"mixA{i}", [128, 4, 512], BF16) for i in range(2)]
        mixA.append(Dgall[:].rearrange("p a b c -> p (a b c)").bitcast(BF16).rearrange("p (k n) -> p k n", k=4))
        mixB = [sb(f"mixB{i}", [128, 4, 512], BF16) for i in range(2)]
        junk = mixB[0][:].rearrange("p a b -> p (a b)")[:, 0:D]
        JUNK = ["mixB0_0", "mixB0_1"]
        DGN = [f"Dg{pb}_{t}" for pb in range(2) for t in range(4)]
        MA2 = [f"mixA2_{c4}" for c4 in range(4)]
        ptmp = [zt, sbg]
        ss2 = sb("ss2", [128, 2])
        rstd2 = sb("rstd2", [128, 1])
        recf = big[:, 0:2048].rearrange("p (c t) -> p c t", c=4)
        ybuf = big[:, 2048:4096].rearrange("p (c t) -> p c t", c=4)
        stage = big[:].rearrange("p (k n) -> p k n", k=8)

        def ACT(out, in_, func, r, w, sync=False, **kw):
            P.op("act", lambda e: e.activation(out=out, in_=in_, func=func, **kw), r, w, sync=sync)

        def MM(out, lhsT, rhs, start, stop, r, w):
            P.op("pe", lambda e: e.matmul(out, lhsT=lhsT, rhs=rhs, start=start, stop=stop), r, w)

        def TS(eng, out, in0, s1, s2, op0, op1, r, w, sync=False):
            if s2 is None:
                P.op(eng, lambda e: e.tensor_scalar(out=out, in0=in0, scalar1=s1, scalar2=None, op0=op0), r, w, sync=sync)
            else:
                P.op(eng, lambda e: e.tensor_scalar(out=out, in0=in0, scalar1=s1, scalar2=s2, op0=op0, op1=op1), r, w, sync=sync)

        def STT(eng, out, in0, scalar, in1, op0, op1, r, w):
            P.op(eng, lambda e: e.scalar_tensor_tensor(out=out, in0=in0, scalar=scalar, in1=in1, op0=op0, op1=op1), r, w)

        def TT(eng, out, in0, in1, op, r, w, sync=False):
            P.op(eng, lambda e: e.tensor_tensor(out=out, in0=in0, in1=in1, op=op), r, w, sync=sync)

        def CP(eng, out, in_, r, w, sync=False):
            P.op(eng, lambda e: e.tensor_copy(out=out, in_=in_), r, w, sync=sync)

        def MS(eng, out, val, w, sync=False):
            P.op(eng, lambda e: e.memset(out, val), (), w, sync=sync)

        def POW(out, in_, which, n, r, w, sync=False):
            ex = cpow[:, which:which + 1].to_broadcast([128, n])
            P.op("pool", lambda e: e.tensor_tensor(out=out, in0=in_, in1=ex, op=ALU.pow), list(r) + ["cpow"], w, sync=sync)

        def DMA(out, in_, r, w):
            P.op("sp", lambda e: e.dma_start(out=out, in_=in_), r, w)

        P.force_sync = True
        DMA(ident[:], ident_d, [], ["ident"])
        DMA(cc[:], cc_d, [], ["cc"])
        CP("pool", identb[:], ident[:], ["ident"], ["identb"])
        MS("pool", onesm[:], 1.0 / 512.0, ["onesm"])
        MS("pool", cq[:], 0.25, ["cq"])
        MS("pool", cpow[:, 0:1], 0.5, ["cpow"])
        MS("pool", cpow[:, 1:2], -0.5, ["cpow"])
        MS("pool", V1pad[:], 0.0, ["V1pad0", "V1pad1"])
        MS("pool", Vc[:], 0.0, ["Vc"])
        ACT(cc[:], cc[:], AF.Silu, ["cc"], ["cc"])
        P.force_sync = False

        BIG = ["big.lo", "big.hi"]

        def load_w_in(l, ocs, groups=(0, 1, 2)):
            for g in groups:
                oc0 = ocs[4 * g]
                src = win_d[l].rearrange("(k p) n -> p k n", p=128)[:, :, oc0 * 128:(oc0 + 4) * 128]
                DMA(stage, src, [], BIG)
                for kc in range(8):
                    if kc % 3 == 2:
                        ACT(w_in_bf[:, kc, g * 512:(g + 1) * 512], stage[:, kc, :], AF.Copy, BIG, ["w_in_bf"])
                    else:
                        CP("dve", w_in_bf[:, kc, g * 512:(g + 1) * 512], stage[:, kc, :], BIG, ["w_in_bf"])

        def load_w_out(l):
            for g in range(2):
                src = wout_d[l].rearrange("(k p) n -> p k n", p=128)[:, :, g * 512:(g + 1) * 512]
                DMA(stage, src, [], BIG)
                for kc in range(8):
                    if kc % 3 == 2:
                        ACT(w_out_bf[:, kc, g * 512:(g + 1) * 512], stage[:, kc, :], AF.Copy, BIG, ["w_out_bf"])
                    else:
                        CP("dve", w_out_bf[:, kc, g * 512:(g + 1) * 512], stage[:, kc, :], BIG, ["w_out_bf"])

        def build_dgm(ccs):
            for i, c4 in enumerate(ccs):
                in0 = identb[:].unsqueeze(1).to_broadcast([128, 31, 128])
                in1 = PPH[:, c4:124:4].unsqueeze(2).to_broadcast([128, 31, 128])
                TT("dve", dgm[:, :, i, :], in0, in1, ALU.mult, ["identb", "PPH"], ["dgm"])

        def layer_setup(l):
            P.force_sync = True
            layer_setup_(l)
            P.force_sync = False

        def layer_setup_(l):
            DMA(PP[:], pp_d[l], [], ["PP"])
            pt, pn = nb()
            for hs in range(12):
                half = big[:, (hs % 2) * 2048:(hs % 2 + 1) * 2048].rearrange("p (k n) -> p k n", k=8)
                hname = BIG[hs % 2]
                src = wmod_d[l].rearrange("(k p) n -> p k n", p=128)[:, :, hs * 256:(hs + 1) * 256]
                P.op("sp", lambda e, half=half, src=src: e.dma_start(out=half, in_=src), [], [hname])
                for o in range(2):
                    oc = hs * 2 + o
                    for kc in range(8):
                        MM(pt[:, oc * 2:oc * 2 + 2], half[:, kc, o * 128:(o + 1) * 128], cc[:, kc * 2:kc * 2 + 2],
                           kc == 0, kc == 7, [hname, "cc"], [pn])
            CP("dve", modT[:], pt[:, 0:48], [pn], ["modT"])
            m3 = modT[:].rearrange("p (o j) -> p o j", j=2)
            for j in range(2):
                a_j = Apre[:].rearrange("p (k j) -> p k j", j=2)[:, :, j]
                b_j = Bpre[:].rearrange("p (k j) -> p k j", j=2)[:, :, j]
                TT("dve", a_j, m3[:, 8:16, j], PP[:, C_BSC:C_BSC + 8], ALU.add, ["modT", "PP"], ["Apre"])
                TS("dve", a_j, a_j, 1.0, None, ALU.add, None, ["Apre"], ["Apre"])
                TT("dve", a_j, a_j, PP[:, C_GPRE:C_GPRE + 8], ALU.mult, ["Apre", "PP"], ["Apre"])
                TT("dve", b_j, m3[:, 0:8, j], PP[:, C_BSH:C_BSH + 8], ALU.add, ["modT", "PP"], ["Bpre"])
                g_j = gT[:, j * 8:(j + 1) * 8]
                TT("dve", g_j, m3[:, 16:24, j], PP[:, C_BGT:C_BGT + 8], ALU.add, ["modT", "PP"], ["gT"])
                TT("dve", g_j, g_j, PP[:, C_GPOST:C_GPOST + 8], ALU.mult, ["gT", "PP"], ["gT"])
            pt, pn = nb()
            MM(pt[0:16, 0:128], gT[:, 0:16], ident[:], True, True, ["gT", "ident"], [pn])
            CP("dve", gTt[:], pt[0:16, 0:128], [pn], ["gTt"])
            DMA(g_d[l].rearrange("j (k p) -> (j k) p", p=128), gTt[:], ["gTt"], ["g_d"])
            DMA(big[:, 0:2048], wg_d[l], [], ["big.lo"])
            CP("dve", wg_bf[:].rearrange("p a b -> p (a b)"), big[:, 0:2048], ["big.lo"], ["wg_bf"])
            ACT(kap[:], PP[:, C_LAM:C_LAM + 8], AF.Exp, ["PP"], ["kap"], scale=-1.0)
            ACT(kap[:], kap[:], AF.Ln, ["kap"], ["kap"], bias=1.0)
            TS("dve", kaph[:], kap[:], -4.0, None, ALU.mult, None, ["kap"], ["kaph"])
            TS("dve", kap[:], kap[:], -8.0, None, ALU.mult, None, ["kap"], ["kap"])
            TS("dve", PPH[:, 0:124], PP[:, C_DWW:C_DWW + 124], 0.5, None, ALU.mult, None, ["PP"], ["PPH"])
            TS("dve", PPH[:, 124:140], PP[:, C_BR:C_BR + 16], 0.5, None, ALU.mult, None, ["PP"], ["PPH"])
            MS("pool", carry[:], 0.0, [f"carry{i}" for i in range(8)])

        src_rows_names = [[]]

        def load_x(src_rows, N, slot):
            nt = N // 128
            P.op("sp", lambda e: e.dma_start(out=xs[slot][:, 0:nt, :], in_=src_rows.rearrange("(t p) d -> p t d", p=128)),
                 src_rows_names[0], [f"xs{slot}"])

        def stats(N, slot, pb):
            nt = N // 128
            X = xs[slot]
            MS("dve", ss[pb][:], 0.0, [f"ss{pb}"], sync=True)
            for t in range(nt):
                ACT(junk, X[:, t, :], AF.Square, [f"xs{slot}"], JUNK + [f"ss{pb}"], accum_out=ss[pb][:, t:t + 1])
            TS("dve", rstd[pb][:, 0:nt], ss[pb][:, 0:nt], 1.0 / D, EPS, ALU.mult, ALU.add, [f"ss{pb}"], [f"rstd{pb}"], sync=True)
            POW(rstd[pb][:, 0:nt], rstd[pb][:, 0:nt], 1, nt, [f"rstd{pb}"], [f"rstd{pb}"], sync=True)
            for t in range(nt):
                TS("dve", Dg[pb][:, t, :], ident[:], rstd[pb][:, t:t + 1], None, ALU.mult, None,
                   ["ident", f"rstd{pb}"], [f"Dg{pb}_{t}"], sync=True)

        def transposes(N, j, slot, pb, store=None):
            nt = N // 128
            X = xs[slot]
            hxT = hxTb[0]
            for kc in range(8):
                pt, pn = nb()
                for t in range(nt):
                    MM(pt[:, t * 128:(t + 1) * 128], X[:, t, kc * 128:(kc + 1) * 128], Dg[pb][:, t, :], True, True,
                       [f"xs{slot}", f"Dg{pb}_{t}"], [pn])
                ACT(hxT[:, kc, 0:N], pt[:, 0:N], AF.Identity, [pn, "Apre", "Bpre"], [f"hxT0_{kc}"],
                    scale=Apre[:, kc * 2 + j:kc * 2 + j + 1], bias=Bpre[:, kc * 2 + j:kc * 2 + j + 1])
            if store is not None:
                dst, dname = store
                DMA(dst, hxT[:, :, 0:N], [f"hxT0_{kc}" for kc in range(8)], [dname])

        def inproj(widx, N, hb=0):
            pt, pn = nb()
            for kc in range(8):
                MM(pt[:, 0:N], w_in_bf[:, kc, widx * 128:(widx + 1) * 128], hxTb[hb][:, kc, 0:N], kc == 0, kc == 7,
                   ["w_in_bf", f"hxT{hb}_{kc}"], [pn])
            return pt, pn

        AVN = [f"av{c}" for c in range(4)]

        def a_branch_val(N):
            for c4 in range(4):
                pt, pn = inproj(c4, N)
                ACT(av[:, c4, 3:3 + N], pt[:, 0:N], AF.Identity, [pn], [f"av{c4}"])

        def lruA_all(d, N):
            base = 0 if d == 0 else 3
            wcol = lambda k, c4: PP[:, C_CAW + d * 16 + k * 4 + c4:C_CAW + d * 16 + k * 4 + c4 + 1]
            for c4 in range(4):
                bcol = PP[:, C_CAB + d * 4 + c4:C_CAB + d * 4 + c4 + 1]
                TS("pool", lvc[c4][:, 0:N], av[:, c4, base:base + N], wcol(0, c4), bcol, ALU.mult, ALU.add,
                   [f"av{c4}", "PP"], [f"lvc{c4}"])
            for c4 in range(4):
                vc = lvc[c4]
                for k in range(1, 4):
                    STT("dve", vc[:, 0:N], av[:, c4, base + k:base + k + N], wcol(k, c4), vc[:, 0:N], ALU.mult, ALU.add,
                        [f"av{c4}", "PP", f"lvc{c4}"], [f"lvc{c4}"])
            for c4 in range(4):
                ACT(lvb[c4][:, 0:N], lvc[c4][:, 0:N], AF.Copy, [f"lvc{c4}"], [f"lvb{c4}"])

        def lruB1(d, c4, N):
            si = c4 % NSET
            S = lt[si]
            nm = lambda k: f"l{k}{si}"
            A, B, C = S["A"], S["B"], S["C"]
            pr, prn = nb()
            MM(pr[:, 0:N], wg_bf[:, (d * 2 + 0) * 4 + c4, :], lvb[c4][:, 0:N], True, True, ["wg_bf", f"lvb{c4}"], [prn])
            pi, pin = nb()
            MM(pi[:, 0:N], wg_bf[:, (d * 2 + 1) * 4 + c4, :], lvb[c4][:, 0:N], True, True, ["wg_bf", f"lvb{c4}"], [pin])
            kc_ = d * 4 + c4
            hb_r = PPH[:, 124 + kc_:125 + kc_]
            hb_i = PPH[:, 132 + kc_:133 + kc_]
            ACT(A[:, 0:N], pr[:, 0:N], AF.Tanh, [prn, "PPH"], [nm("A")], scale=0.5, bias=hb_r)
            ACT(B[:, 0:N], pi[:, 0:N], AF.Tanh, [pin, "PPH"], [nm("B")], scale=0.5, bias=hb_i)
            ACT(C[:, 0:N], A[:, 0:N], AF.Exp, [nm("A"), "kap"], [nm("C")], scale=kap[:, kc_:kc_ + 1], bias=kap[:, kc_:kc_ + 1])
            ACT(A[:, 0:N], A[:, 0:N], AF.Exp, [nm("A"), "kaph"], [nm("A")], scale=kaph[:, kc_:kc_ + 1], bias=kaph[:, kc_:kc_ + 1])

        def lruBs(d, c4, N):
            si = c4 % NSET
            C = lt[si]["C"]
            ACT(C[:, 0:N], C[:, 0:N], AF.Sqrt, [f"lC{si}", "cq"], [f"lC{si}"], scale=-0.25, bias=cq[:, 0:1])

        def lruB2(d, c4, N):
            si = c4 % NSET
            S = lt[si]
            nm = lambda k: f"l{k}{si}"
            vc, A, B, C, H = lvc[c4], S["A"], S["B"], S["C"], S["H"]
            kc_ = d * 4 + c4
            STT("dve", B[:, 0:N], B[:, 0:N], 1.0, vc[:, 0:N], ALU.add, ALU.mult, [nm("B"), f"lvc{c4}"], [nm("B")])
            TT("dve", B[:, 0:N], B[:, 0:N], C[:, 0:N], ALU.mult, [nm("B"), nm("C")], [nm("B")])
            cin = carry[:, kc_:kc_ + 1]
            if d == 0:
                P.op("dve", lambda e: e.tensor_tensor_scan(out=H[:, 0:N], data0=A[:, 0:N], data1=B[:, 0:N], initial=cin,
                                                           op0=ALU.mult, op1=ALU.add),
                     [nm("A"), nm("B"), f"carry{kc_}"], [nm("H")])
                CP("pool", cin, H[:, N - 1:N], [nm("H")], [f"carry{kc_}"], sync=True)
            else:
                P.op("dve", lambda e: e.tensor_tensor_scan(out=H[:, 0:N][:, ::-1], data0=A[:, 0:N][:, ::-1],
                                                           data1=B[:, 0:N][:, ::-1], initial=cin,
                                                           op0=ALU.mult, op1=ALU.add),
                     [nm("A"), nm("B"), f"carry{kc_}"], [nm("H")], sync=True)
                CP("pool", cin, H[:, 0:1], [nm("H")], [f"carry{kc_}"], sync=True)
            return H, nm("H")

        def lru_all(d, N, consume, extra_sqrt=lambda: None, inter=lambda k: None, hook=lambda tag: None):
            for half in range(2):
                c4s = (2 * half, 2 * half + 1)
                for c4 in c4s:
                    lruB1(d, c4, N)
                hook(f"b1_{half}")
                for c4 in c4s:
                    lruBs(d, c4, N)
                if half == 0:
                    extra_sqrt()
                for c4 in c4s:
                    inter(c4)
                    H, hn = lruB2(d, c4, N)
                    consume(c4, H, hn)
                hook(f"done_{half}")

        vt_ctr = [0]

        def pass1_chunk(l, c, nxt, snx):
            isctx = c < 0
            N = TC if isctx else 512
            full = (not isctx) or (l == 0)
            a_branch_val(N)
            if isctx:
                DMA(avc_d.rearrange("(c p) t -> p c t", p=128), av[:, :, 3:3 + N], AVN, ["avd-1"])
            else:
                DMA(av_d.rearrange("(c p) t -> p c t", p=128)[:, :, c * 512:(c + 1) * 512], av[:, :, 3:3 + N], AVN, [f"avd{c}"])
            lruA_all(0, N)
            snx()
            if full:
                for c4 in range(4):
                    pv, pvn = inproj(4 + c4, N)
                    pg, pgn = inproj(8 + c4, N)
                    ACT(sg[:, 0:N], pg[:, 0:N], AF.Tanh, [pgn], ["sg"], scale=0.5)
                    if isctx:
                        STT("dve", Vc[:, c4, 15:15 + N], sg[:, 0:N], 1.0, pv[:, 0:N], ALU.add, ALU.mult, [pvn, "sg"], ["Vc"])
                    elif c4 < 2:
                        STT("dve", V1pad[:, c4, :, 15:79], sg[:, 0:N].rearrange("p (r w) -> p r w", w=64), 1.0,
                            pv[:, 0:N].rearrange("p (r w) -> p r w", w=64), ALU.add, ALU.mult, [pvn, "sg"], [f"V1pad{c4}"])
                    else:
                        vi = vt_ctr[0] % 2
                        vt_ctr[0] += 1
                        STT("dve", vtmp[vi][:], sg[:, 0:N], 1.0, pv[:, 0:N], ALU.add, ALU.mult, [pvn, "sg"], ["vtmp0"])
                        DMA(v2_d[(c4 - 2) * 128:(c4 - 1) * 128, c * 512:(c + 1) * 512], vtmp[vi][:], ["vtmp0"], [f"v2d{c}_{c4}"])
                for c4 in range(2):
                    pt, pn = nb()
                    for k in range(31):
                        if isctx:
                            MM(pt[:, 0:N], dgm[:, k, c4, :], Vc[:, c4, k:k + N], k == 0, k == 30, ["dgm", "Vc"], [pn])
                        else:
                            MM(pt[:].rearrange("p (r w) -> p r w", w=64), dgm[:, k, c4, :], V1pad[:, c4, :, k:k + 64],
                               k == 0, k == 30, ["dgm", f"V1pad{c4}"], [pn])
                    bcol = PP[:, C_DWB + c4:C_DWB + c4 + 1]
                    if isctx:
                        ACT(ctx_y[:, c4, :], pt[:, 0:N], AF.Identity, [pn, "PP"], ["ctx_y"], bias=bcol)
                    else:
                        ACT(y1buf[c4][:], pt[:], AF.Identity, [pn, "PP"], [("ysq", "mean")[c4]], bias=bcol)
                        DMA(y1_d[c4 * 128:(c4 + 1) * 128, c * 512:(c + 1) * 512], y1buf[c4][:], [("ysq", "mean")[c4]], [f"y1d{c}_{c4}"])
            def consume1(c4, H, hn):
                if isctx:
                    if l == 0:
                        CP("pool", ctx_recf[:, c4, :], H[:, 0:N], [hn], ["ctx_recf"])
                else:
                    DMA(recf_d[c4 * 128:(c4 + 1) * 128, c * 512:(c + 1) * 512], H[:, 0:N], [hn], [f"recfd{c}_{c4}"])

            lru_all(0, N, consume1)
            CP("pool", av[:, :, 0:3], av[:, :, N:N + 3], AVN, AVN, sync=True)
            nxt()

        def p2_conv(i, N, isctx):
            c4 = 2 + i
            pt, pn = nb()
            for k in range(31):
                if isctx:
                    MM(pt[:, 0:N], dgm[:, k, i, :], Vc[:, c4, k:k + N], k == 0, k == 30, ["dgm", "Vc"], [pn])
                else:
                    MM(pt[:, 0:N], dgm[:, k, i, :], V2win[:, i, k * 64:k * 64 + 512], k == 0, k == 30,
                       ["dgm", "V2win"], [pn])
            ACT(ybuf[:, c4, 0:N], pt[:, 0:N], AF.Identity, [pn, "PP"], ["big.hi"], bias=PP[:, C_DWB + c4:C_DWB + c4 + 1])

        def p2_stats(N, isctx):
            if isctx:
                CP("pool", ybuf[:, 0:2, 0:N], ctx_y[:], ["ctx_y"], ["big.hi"])
            pm, pmn = nb()
            for c4 in range(4):
                MM(pm[:, 0:N], onesm[:], ybuf[:, c4, 0:N], c4 == 0, c4 == 3, ["onesm", "big.hi"], [pmn])
            pq, pqn = nb()
            for c4 in range(4):
                yq, yqn = ((ysq, "ysq"), (ysqB, "ysqB"))[c4 % 2]
                ACT(yq[:, 0:N], ybuf[:, c4, 0:N], AF.Square, ["big.hi"], [yqn])
                MM(pq[:, 0:N], onesm[:], yq[:, 0:N], c4 == 0, c4 == 3, ["onesm", yqn], [pqn])
            CP("dve", mean[:, 0:N], pm[:, 0:N], [pmn], ["mean"])
            TT("dve", rs_ln[:, 0:N], mean[:, 0:N], mean[:, 0:N], ALU.mult, ["mean"], ["rs_ln"])
            TT("dve", rs_ln[:, 0:N], pq[:, 0:N], rs_ln[:, 0:N], ALU.subtract, [pqn, "rs_ln"], ["rs_ln"])
            TS("dve", rs_ln[:, 0:N], rs_ln[:, 0:N], EPS, None, ALU.add, None, ["rs_ln"], ["rs_ln"])

        def p2_ln_sqrt(N):
            ACT(rs_ln[:, 0:N], rs_ln[:, 0:N], AF.Sqrt, ["rs_ln"], ["rs_ln"])
            P.op("dve", lambda e: e.reciprocal(out=rs_ln[:, 0:N], in_=rs_ln[:, 0:N]), ["rs_ln"], ["rs_ln"])

        def p2_ln_iter(c4, N, hb, mb):
            b = c4 % 2
            p_, pn1 = ((ysq, "ysq"), (ysqB, "ysqB"))[b]
            TT("pool", zt[:, 0:N], ybuf[:, c4, 0:N], mean[:, 0:N], ALU.subtract, ["big.hi", "mean"], ["zt"])
            TT("dve", zt[:, 0:N], zt[:, 0:N], rs_ln[:, 0:N], ALU.mult, ["zt", "rs_ln"], ["zt"])
            ACT(zt[:, 0:N], zt[:, 0:N], AF.Identity, ["zt", "PP"], ["zt"],
                scale=PP[:, C_LNG + c4:C_LNG + c4 + 1], bias=PP[:, C_LNB + c4:C_LNB + c4 + 1])
            ACT(p_[:, 0:N], zt[:, 0:N], AF.Tanh, ["zt"], [pn1], scale=0.5)
            pt, pn = inproj(8 + c4, N, hb)
            ACT(sbg[:, 0:N], pt[:, 0:N], AF.Tanh, [pn], ["sbg"], scale=0.5)
            STT("dve", p_[:, 0:N], p_[:, 0:N], 1.0, zt[:, 0:N], ALU.add, ALU.mult, [pn1, "zt"], [pn1])
            STT("dve", sbg[:, 0:N], sbg[:, 0:N], 1.0, pt[:, 0:N], ALU.add, ALU.mult, [pn, "sbg"], ["sbg"])
            STT("dve", mixB[mb][:, c4, 0:N], p_[:, 0:N], 0.25, sbg[:, 0:N], ALU.mult, ALU.mult, [pn1, "sbg"], [f"mixB{mb}_{c4}"])

        def p2_lru(l, cn, k, extra, inter, hook=lambda tag: None):
            isctx = cn < 0
            N = TC if isctx else 512
            full = (not isctx) or (l == 0)
            hb = k % 2
            par = k % 3

            def consume2(c4, H, hn):
                if not full:
                    return
                pt, pn = inproj(4 + c4, N, hb)
                ACT(sag[:, 0:N], pt[:, 0:N], AF.Tanh, [pn], ["sg"], scale=0.5)
                STT("dve", sag[:, 0:N], sag[:, 0:N], 1.0, pt[:, 0:N], ALU.add, ALU.mult, [pn, "sg"], ["sg"])
                rf = ctx_recf[:, c4, :] if isctx else recf[:, c4, :]
                rfn = "ctx_recf" if isctx else "big.lo"
                TT("pool", H[:, 0:N], H[:, 0:N], rf, ALU.add, [hn, rfn], [hn])
                STT("dve", mixA[par][:, c4, 0:N], H[:, 0:N], 0.5, sag[:, 0:N], ALU.mult, ALU.mult, [hn, "sg"], [f"mixA{par}_{c4}"])

            lru_all(1, N, consume2, extra, inter, hook)

        def p2_out_tile(k, t, dst_rows, dst_name):
            X = xs[0]
            pa, pb_ = k % 3, k % 2
            halves = []
            MS("dve", ss2[:], 0.0, ["ss2"], sync=True)
            for h in range(2):
                pt, pn = nb()
                for kc in range(8):
                    lhs = mixA[pa][:, kc, t * 128:(t + 1) * 128] if kc < 4 else mixB[pb_][:, kc - 4, t * 128:(t + 1) * 128]
                    lname = f"mixA{pa}_{kc}" if kc < 4 else f"mixB{pb_}_{kc - 4}"
                    MM(pt[:], lhs, w_out_bf[:, kc, h * 512:(h + 1) * 512], kc == 0, kc == 7, [lname, "w_out_bf"], [pn])
                ACT(vtmp[0][:], pt[:], AF.Square, [pn], ["vtmp0", "ss2"], accum_out=ss2[:, h:h + 1])
                halves.append((pt, pn))
            TT("dve", rstd2[:], ss2[:, 0:1], ss2[:, 1:2], ALU.add, ["ss2"], ["rstd2"], sync=True)
            TS("dve", rstd2[:], rstd2[:], 1.0 / D, EPS, ALU.mult, ALU.add, ["rstd2"], ["rstd2"], sync=True)
            POW(rstd2[:], rstd2[:], 1, 1, ["rstd2"], ["rstd2"], sync=True)
            for h, (pt, pn) in enumerate(halves):
                STT("dve", ptmp[h][:], pt[:], rstd2[:, 0:1], G[:, h * 512:(h + 1) * 512], ALU.mult, ALU.mult,
                    [pn, "rstd2", "G"], [("zt", "sbg")[h]])
                TT("pool", X[:, t, h * 512:(h + 1) * 512], ptmp[h][:], X[:, t, h * 512:(h + 1) * 512], ALU.add,
                   [("zt", "sbg")[h], f"xr{t}"], [f"xr{t}"])
            deferred_stores.append(lambda: DMA(dst_rows[t * 128:(t + 1) * 128, :], X[:, t, :], [f"xr{t}"], [f"{dst_name}_{t}"]))

        deferred_stores = []

        XR = [f"xr{t}" for t in range(4)]
        for l in range(L):
            xin = x_d if l == 0 else x1_d
            xout = x1_d if l == 0 else out_d
            cin_d = ctx_d if l == 0 else xc1_d
            xname = (lambda c: []) if l == 0 else (lambda c: [f"x1_{c}_{t}" for t in range(4)])
            cname = [] if l == 0 else ["xc1_0", "xc1_1"]
            oname = (lambda c: f"x1_{c}") if l == 0 else (lambda c: f"out_{c}")

            def rows(c):
                return cin_d if c < 0 else xin[c * 512:(c + 1) * 512, :]

            def rnames(c):
                return cname if c < 0 else xname(c)

            def ld(c):
                src_rows_names[0] = rnames(c)
                load_x(rows(c), TC if c < 0 else 512, 0)

            def NJ(c):
                return (TC, 1) if c < 0 else (512, 0)

            def hx_store(c):
                if c < 0:
                    return (hxc_d.rearrange("(k p) t -> p k t", p=128), "hxd-1")
                return (hx_d.rearrange("(k p) t -> p k t", p=128)[:, :, c * 512:(c + 1) * 512], f"hxd{c}")

            layer_setup(l)
            load_w_in(l, [0, 1, 2, 3, 8, 9, 10, 11, 12, 13, 14, 15])
            build_dgm([0, 1])
            MS("pool", av[:], 0.0, AVN)
            P.op("pool", lambda e: e.memset(dummy[:, 0:1], 0.0), XR + MA2, ["xs0"] + DGN)
            seq = [-1] + list(range(NCH))
            ld(seq[0])
            n0, j0 = NJ(seq[0])
            stats(n0, 0, 0)
            transposes(n0, j0, 0, 0, hx_store(seq[0]))
            for i, c in enumerate(seq):
                if i + 1 < len(seq):
                    c1 = seq[i + 1]
                    n1, j1 = NJ(c1)
                    p1 = (i + 1) % 2
                    ld(c1)
                    snx = (lambda n1=n1, p1=p1: stats(n1, 0, p1))
                    nxt = (lambda n1=n1, j1=j1, p1=p1, c1=c1: transposes(n1, j1, 0, p1, hx_store(c1)))
                else:
                    snx = (lambda: None)
                    nxt = (lambda: None)
                pass1_chunk(l, c, nxt, snx)
                if c < 0:
                    MS("pool", av[:], 0.0, AVN)
                if i == 5:
                    load_w_out(l)
            load_w_in(l, [0, 1, 2, 3, 4, 5, 6, 7, 16, 17, 18, 19], groups=(1, 2))
            build_dgm([2, 3])
            MS("pool", av[:], 0.0, AVN)
            P.op("pool", lambda e: e.memset(dummy[:, 1:2], 0.0), ["xs0"] + DGN, XR + MA2)
            if l == 0:
                DMA(G[:], g_d[l, 1].partition_broadcast(128), ["g_d"], ["G"])
            else:
                DMA(G[:], g_d[l, 0].partition_broadcast(128), ["g_d"], ["G"])

            def loads_A(c, hb):
                n = TC if c < 0 else 512
                full_c = (c >= 0) or (l == 0)
                if c >= 0:
                    DMA(recf, recf_d.rearrange("(c p) t -> p c t", p=128)[:, :, c * 512:(c + 1) * 512], [f"recfd{c}_{q}" for q in range(4)], ["big.lo"])
                if full_c:
                    src = hxc_d.rearrange("(k p) t -> p k t", p=128) if c < 0 else \
                        hx_d.rearrange("(k p) t -> p k t", p=128)[:, :, c * 512:(c + 1) * 512]
                    DMA(hxTb[hb][:, :, 0:n], src, [f"hxd{c}"], [f"hxT{hb}_{kc}" for kc in range(8)])
                if c < 0:
                    DMA(av[:, :, 3:3 + n], avc_d.rearrange("(c p) t -> p c t", p=128), ["avd-1"], AVN)
                elif c == NCH - 1:
                    MS("pool", av[:, :, 3 + n:6 + n], 0.0, AVN)
                    DMA(av[:, :, 3:3 + n], av_d.rearrange("(c p) t -> p c t", p=128)[:, :, c * 512:(c + 1) * 512], [f"avd{c}"], AVN)
                else:
                    DMA(av[:, :, 3:6 + n], av_d.rearrange("(c p) t -> p c t", p=128)[:, :, c * 512:(c + 1) * 512 + 3],
                        [f"avd{c}", f"avd{c + 1}"], AVN)

            def loads_B(c):
                DMA(ybuf[:, 0:2, :], y1_d.rearrange("(c p) t -> p c t", p=128)[:, :, c * 512:(c + 1) * 512], [f"y1d{c}_{q}" for q in range(2)], ["big.hi"])
                r0 = 8 * c - 15
                lo, hi = max(0, r0), min(128, r0 + 38)
                if lo > r0 or hi < r0 + 38:
                    MS("pool", V2win[:], 0.0, ["V2win"])
                DMA(V2win[:, :, (lo - r0) * 64:(hi - r0) * 64],
                    v2_d.rearrange("(c p) t -> p c t", p=128)[:, :, lo * 64:hi * 64],
                    [f"v2d{q}_{r}" for q in range(max(0, c - 2), min(NCH, c + 3)) for r in (2, 3)], ["V2win"])

            seq = [-1] + list(range(NCH - 1, -1, -1))
            nseq = len(seq)
            isfull = lambda c: (c >= 0) or (l == 0)
            loads_A(seq[0], 0)
            lruA_all(1, TC)
            p2_lru(l, seq[0], 0, lambda: None, lambda k: None)
            loads_A(seq[1], 1)
            lruA_all(1, 512)
            for i in range(nseq + 1):
                c = seq[i] if i < nseq else None
                cn = seq[i + 1] if i + 1 < nseq else None
                cnn = seq[i + 2] if i + 2 < nseq else None
                cp = seq[i - 1] if i >= 1 else None
                doB = c is not None and isfull(c)
                doO = cp is not None and isfull(cp)
                Nc = TC if (c is not None and c < 0) else 512
                Np = TC if (cp is not None and cp < 0) else 512
                hb = i % 2
                if doO:
                    for t in range(Np // 128):
                        DMA(xs[0][:, t, :], rows(cp)[t * 128:(t + 1) * 128, :], rnames(cp), [f"xr{t}"])
                    orow = xc1_d if cp < 0 else xout[cp * 512:(cp + 1) * 512, :]
                    onm = "xc1" if cp < 0 else oname(cp)

                def hook(tag, c=c, cp=cp, doB=doB, doO=doO, Nc=Nc, Np=Np, i=i):
                    order = ["b1_0", "done_0", "b1_1", "done_1"]
                    t = order.index(tag)
                    if doB and t < 2:
                        p2_conv(t, Nc, c < 0)
                    if doO:
                        for tt in ((0, 1), (2,), (3,), ())[t]:
                            if tt < Np // 128:
                                p2_out_tile(i - 1, tt, orow, onm)
                    if doB and t == 2:
                        p2_stats(Nc, c < 0)

                if cn is not None:
                    p2_lru(l, cn, i + 1, lambda: None, lambda k: None, hook)
                else:
                    for tag in ("b1_0", "done_0", "b1_1", "done_1"):
                        hook(tag)
                if doO and cp < 0:
                    DMA(G[:], g_d[l, 0].partition_broadcast(128), ["g_d"], ["G"])
                if doB:
                    p2_ln_sqrt(Nc)
                    for k in range(4):
                        p2_ln_iter(k, Nc, hb, i % 2)
                if cnn is not None:
                    loads_A(cnn, (i + 2) % 2)
                    lruA_all(1, 512)
                if cn is not None and cn >= 0:
                    loads_B(cn)
                for st_ in deferred_stores:
                    st_()
                deferred_stores.clear()

        P.emit()
    return nc


def _pack_inputs(inp, b):
    f = lambda a: np.ascontiguousarray(np.asarray(a, dtype=np.float32))
    fm = lambda v, n: v.reshape(n, 128).T
    cc = np.zeros((128, 16), np.float32)
    cc[:, 0::2] = fm(f(inp["c"])[b], 8)
    cc[:, 1::2] = fm(f(inp["c_ctx"]), 8)
    pp = np.zeros((L, 128, NPP), np.float32)
    wg = np.zeros((L, 128, 16, 128), np.float32)
    for l in range(L):
        bm = f(inp["b_mod"])[l]
        pp[l, :, C_GPRE:C_GPRE + 8] = fm(f(inp["g_pre"])[l], 8)
        pp[l, :, C_BSH:C_BSH + 8] = fm(bm[0:D], 8)
        pp[l, :, C_BSC:C_BSC + 8] = fm(bm[D:2 * D], 8)
        pp[l, :, C_BGT:C_BGT + 8] = fm(bm[2 * D:3 * D], 8)
        pp[l, :, C_GPOST:C_GPOST + 8] = fm(f(inp["g_post"])[l], 8)
        for d in range(2):
            for k in range(4):
                pp[l, :, C_CAW + d * 16 + k * 4:C_CAW + d * 16 + k * 4 + 4] = fm(f(inp["conv_a_w"])[l, d, k], 4)
            pp[l, :, C_CAB + d * 4:C_CAB + d * 4 + 4] = fm(f(inp["conv_a_b"])[l, d], 4)
            pp[l, :, C_BR + d * 4:C_BR + d * 4 + 4] = fm(f(inp["b_rgate"])[l, d], 4)
            pp[l, :, C_BI + d * 4:C_BI + d * 4 + 4] = fm(f(inp["b_igate"])[l, d], 4)
            pp[l, :, C_LAM + d * 4:C_LAM + d * 4 + 4] = fm(f(inp["lru_lambda"])[l, d], 4)
            for gi, key in enumerate(("w_rgate", "w_igate")):
                w = f(inp[key])[l, d]
                for c4 in range(4):
                    for hh in range(2):
                        wg[l, hh * 64:(hh + 1) * 64, (d * 2 + gi) * 4 + c4, hh * 64:(hh + 1) * 64] = w[c4 * 2 + hh]
        for k in range(31):
            pp[l, :, C_DWW + k * 4:C_DWW + k * 4 + 4] = fm(f(inp["dw_w"])[l, k], 4)
        pp[l, :, C_DWB:C_DWB + 4] = fm(f(inp["dw_b"])[l], 4)
        pp[l, :, C_LNG:C_LNG + 4] = fm(f(inp["ln_g"])[l], 4)
        pp[l, :, C_LNB:C_LNB + 4] = fm(f(inp["ln_b"])[l], 4)
    return {
        "x": f(inp["x"])[b], "ctx": f(inp["ctx"])[b], "cc": cc, "pp": pp,
        "w_mod": f(inp["w_mod"]), "w_in": f(inp["w_in"]), "w_out": f(inp["w_out"]),
        "wg": wg.reshape(L, 128, 16 * 128), "ident": np.eye(128, dtype=np.float32),
    }


_NC_CACHE = {}


def kernel(**inputs):
    if "nc" not in _NC_CACHE:
        _NC_CACHE["nc"] = build_program()
    nc = _NC_CACHE["nc"]
    maps = [_pack_inputs(inputs, b) for b in range(2)]
    in_maps = [maps[r % 2] for r in range(8)]
    res = run_bass_kernel_spmd(nc, in_maps, core_ids=list(range(8)))
    out = np.stack([np.asarray(res.results[b]["out"], dtype=np.float32) for b in range(2)], axis=0)
    if DEBUG_OUT:
        kernel.debug = [{k: np.asarray(res.results[b][k]) for k in ("x1", "xc1")} for b in range(2)]
    return out
```

```python
import contextlib
import numpy as np
import concourse.bass as bass
import concourse.mybir as mybir
from concourse.bass_utils import run_bass_kernel_spmd

F32 = mybir.dt.float32
BF16 = mybir.dt.bfloat16
AF = mybir.ActivationFunctionType
ALU = mybir.AluOpType

D = 1024
T = 8192
TC = 256
L = 2
NCH = T // 512
EPS = 1e-6
NP = 224
C_GPRE, C_BSH, C_BSC, C_BGT, C_GPOST = 0, 8, 16, 224, 232
C_CAW, C_CAB, C_BR, C_BI, C_LAM = 24, 56, 64, 72, 80
C_DWW, C_DWB, C_LNG, C_LNB = 88, 212, 216, 220
NPP = 240

COMPUTE = ("pe", "act", "dve", "pool")
DMAQ = ("sp",)
NDSEM = 16
SAME_ENGINE_SYNC = False
DEBUG_OUT = False


class Prog:
    def __init__(self, nc):
        self.nc = nc
        self.engs = COMPUTE + DMAQ
        self.ops = {e: [] for e in self.engs}
        self.count = {e: 0 for e in self.engs}
        self.last_write = {}
        self.readers = {}
        self.seen = {e: {e2: 0 for e2 in COMPUTE} for e in self.engs}
        self.seen_dma = {e: set() for e in self.engs}
        self.force_sync = False
        self.sync_ops = set()

    def op(self, eng, fn, reads=(), writes=(), sync=False):
        idx = self.count[eng] + 1
        self.count[eng] = idx
        sync = sync or self.force_sync
        if sync:
            self.sync_ops.add((eng, idx))
        deps = set()
        for b in reads:
            if b in self.last_write:
                deps.add(self.last_write[b])
        for b in writes:
            if b in self.last_write:
                deps.add(self.last_write[b])
            deps.update(self.readers.get(b, ()))
        waits = {}
        dwaits = set()
        for (e2, i2) in deps:
            if e2 == eng and (eng == "pe" or not (SAME_ENGINE_SYNC or sync or (e2, i2) in self.sync_ops)):
                continue
            if e2 in DMAQ:
                if (e2, i2) in self.seen_dma[eng]:
                    continue
                dwaits.add((e2, i2))
            else:
                if self.seen[eng][e2] >= i2:
                    continue
                waits[e2] = max(waits.get(e2, 0), i2)
        for e2, i2 in waits.items():
            self.seen[eng][e2] = i2
        for d in dwaits:
            self.seen_dma[eng].add(d)
        self.ops[eng].append((fn, waits, dwaits, idx))
        me = (eng, idx)
        for b in writes:
            self.last_write[b] = me
            self.readers[b] = []
        for b in reads:
            if b not in writes:
                self.readers.setdefault(b, []).append(me)
        return me

    def emit(self):
        nc = self.nc
        with contextlib.ExitStack() as st:
            sems = {e: st.enter_context(nc.semaphore("sem_" + e)) for e in COMPUTE}
            dsems = {e: [st.enter_context(nc.semaphore(f"dsem_{e}_{i}")) for i in range(NDSEM)] for e in DMAQ}
            block = st.enter_context(nc.Block())
            handles = {"pe": "tensor", "act": "scalar", "dve": "vector", "pool": "gpsimd", "sp": "sync"}

            def make(eng):
                def body(e):
                    for (fn, waits, dwaits, idx) in self.ops[eng]:
                        for e2, i2 in waits.items():
                            e.wait_ge(sems[e2], i2)
                        for (e2, i2) in dwaits:
                            k = i2 - 1
                            e.wait_ge(dsems[e2][k % NDSEM], 16 * (k // NDSEM + 1))
                        if eng in DMAQ and idx - 1 >= NDSEM:
                            k = idx - 1
                            e.wait_ge(dsems[eng][k % NDSEM], 16 * (k // NDSEM))
                        ins = fn(e)
                        if eng in DMAQ:
                            k = idx - 1
                            ins.then_inc(dsems[eng][k % NDSEM], 16)
                        else:
                            ins.then_inc(sems[eng], 1)
                    if eng in DMAQ:
                        n = self.count[eng]
                        for s in range(NDSEM):
                            cnt = len(range(s, n, NDSEM))
                            if cnt:
                                e.wait_ge(dsems[eng][s], 16 * cnt)
                return body

            for eng in self.engs:
                getattr(block, handles[eng])(make(eng))


def build_program():
    nc = bass.Bass("TRN2", target_bir_lowering=False)

    def dram(name, shape, dt=F32, kind="ExternalInput"):
        return nc.dram_tensor(name, shape, dt, kind=kind).ap()

    x_d = dram("x", [T, D])
    ctx_d = dram("ctx", [TC, D])
    cc_d = dram("cc", [128, 16])
    pp_d = dram("pp", [L, 128, NPP])
    wmod_d = dram("w_mod", [L, D, 3 * D])
    win_d = dram("w_in", [L, D, 2560])
    wout_d = dram("w_out", [L, D, D])
    wg_d = dram("wg", [L, 128, 16 * 128])
    ident_d = dram("ident", [128, 128])
    out_d = dram("out", [T, D], kind="ExternalOutput")
    skind = "ExternalOutput" if DEBUG_OUT else "Internal"
    x1_d = dram("x1", [T, D], kind=skind)
    xc1_d = dram("xc1", [TC, D], kind=skind)
    recf_d = dram("recf", [512, T], kind="Internal")
    y1_d = dram("y1", [256, T], kind="Internal")
    v2_d = dram("v2", [256, T], BF16, kind="Internal")
    g_d = dram("gsc", [L, 2, D], kind="Internal")
    hx_d = dram("hx", [D, T], BF16, kind="Internal")
    hxc_d = dram("hxc", [D, TC], BF16, kind="Internal")
    av_d = dram("avs", [512, T], kind="Internal")
    avc_d = dram("avcs", [512, TC], kind="Internal")

    with contextlib.ExitStack() as st:
        def sb(name, shape, dt=F32):
            return st.enter_context(nc.sbuf_tensor("s_" + name, shape, dt))

        ps = [st.enter_context(nc.psum_tensor(f"ps{i}", [128, 512], F32)) for i in range(8)]
        ps_ctr = [0]

        def nb():
            i = ps_ctr[0] % 8
            ps_ctr[0] += 1
            return ps[i], f"ps{i}"

        P = Prog(nc)

        ident = sb("ident", [128, 128])
        identb = sb("identb", [128, 128], BF16)
        onesm = sb("onesm", [128, 128])
        cc = sb("cc", [128, 16])
        PP = sb("PP", [128, NPP])
        modT = sb("modT", [128, 48])
        Apre = sb("Apre", [128, 16])
        Bpre = sb("Bpre", [128, 16])
        gT = sb("gT", [128, 16])
        gTt = sb("gTt", [16, 128])
        G = sb("G", [128, D])
        PPH = sb("PPH", [128, 140])
        kaph = sb("kaph", [128, 8])
        cq = sb("cq", [128, 1])
        cpow = sb("cpow", [128, 2])
        kap = sb("kap", [128, 8])
        carry = sb("carry", [128, 8])
        big = sb("big", [128, 4096])
        w_in_bf = sb("w_in_bf", [128, 8, 12 * 128], BF16)
        w_out_bf = sb("w_out_bf", [128, 8, D], BF16)
        wg_bf = sb("wg_bf", [128, 16, 128], BF16)
        dgm = sb("dgm", [128, 31, 2, 128], BF16)
        xs = [sb("xs0", [128, 4, D])]
        hxTb = [sb("hxT", [128, 8, 512], BF16), sb("hxT2", [128, 8, 512], BF16)]
        dummy = sb("dummy", [128, 2])

        ss = [sb(f"ss{i}", [128, 4]) for i in range(2)]
        rstd = [sb(f"rstd{i}", [128, 4]) for i in range(2)]
        Dgall = sb("Dgall", [128, 2, 4, 128])
        Dg = [Dgall[:, i] for i in range(2)]
        av = sb("av", [128, 4, 518])
        NSET = 2
        lt = [{k: sb(f"l{k}{i}", [128, 512]) for k in ("A", "B", "C", "H")} for i in range(NSET)]
        lvc = [sb(f"lvc{i}", [128, 512]) for i in range(4)]
        lvb = [sb(f"lvb{i}", [128, 512], BF16) for i in range(4)]
        sg = sb("sg", [128, 512])
        sag = sg
        vtmp = [sb("vtmp0", [128, 512], BF16)] * 2
        V1pad = sb("V1pad", [128, 2, 8, 94], BF16)
        V2win = sb("V2win", [128, 2, 38 * 64], BF16)
        Vc = sb("Vc", [128, 4, TC + 30], BF16)

        ctx_y = sb("ctx_y", [128, 2, TC])
        ctx_recf = sb("ctx_recf", [128, 4, TC])
        ysq = sb("ysq", [128, 512])
        mean = sb("mean", [128, 512])
        rs_ln = sb("rs_ln", [128, 512])
        zt = sb("zt", [128, 512])
        sbg = sb("sbg", [128, 512])
        ysqB = sb("ysqB", [128, 512])
        y1buf = [ysq, mean]
        mixA = [sb(f"mixA{i}", [128, 4, 512], BF16) for i in range(2)]
        mixA.append(Dgall[:].rearrange("p a b c -> p (a b c)").bitcast(BF16).rearrange("p (k n) -> p k n", k=4))
        mixB = [sb(f"mixB{i}", [128, 4, 512], BF16) for i in range(2)]
        junk = mixB[0][:].rearrange("p a b -> p (a b)")[:, 0:D]
        JUNK = ["mixB0_0", "mixB0_1"]
        DGN = [f"Dg{pb}_{t}" for pb in range(2) for t in range(4)]
        MA2 = [f"mixA2_{c4}" for c4 in range(4)]
        ptmp = [zt, sbg]
        ss2 = sb("ss2", [128, 2])
        rstd2 = sb("rstd2", [128, 1])
        recf = big[:, 0:2048].rearrange("p (c t) -> p c t", c=4)
        ybuf = big[:, 2048:4096].rearrange("p (c t) -> p c t", c=4)
        stage = big[:].rearrange("p (k n) -> p k n", k=8)

        def ACT(out, in_, func, r, w, sync=False, **kw):
            P.op("act", lambda e: e.activation(out=out, in_=in_, func=func, **kw), r, w, sync=sync)

        def MM(out, lhsT, rhs, start, stop, r, w):
            P.op("pe", lambda e: e.matmul(out, lhsT=lhsT, rhs=rhs, start=start, stop=stop), r, w)

        def TS(eng, out, in0, s1, s2, op0, op1, r, w, sync=False):
            if s2 is None:
                P.op(eng, lambda e: e.tensor_scalar(out=out, in0=in0, scalar1=s1, scalar2=None, op0=op0), r, w, sync=sync)
            else:
                P.op(eng, lambda e: e.tensor_scalar(out=out, in0=in0, scalar1=s1, scalar2=s2, op0=op0, op1=op1), r, w, sync=sync)

        def STT(eng, out, in0, scalar, in1, op0, op1, r, w):
            P.op(eng, lambda e: e.scalar_tensor_tensor(out=out, in0=in0, scalar=scalar, in1=in1, op0=op0, op1=op1), r, w)

        def TT(eng, out, in0, in1, op, r, w, sync=False):
            P.op(eng, lambda e: e.tensor_tensor(out=out, in0=in0, in1=in1, op=op), r, w, sync=sync)

        def CP(eng, out, in_, r, w, sync=False):
            P.op(eng, lambda e: e.tensor_copy(out=out, in_=in_), r, w, sync=sync)

        def MS(eng, out, val, w, sync=False):
            P.op(eng, lambda e: e.memset(out, val), (), w, sync=sync)

        def POW(out, in_, which, n, r, w, sync=False):
            ex = cpow[:, which:which + 1].to_broadcast([128, n])
            P.op("pool", lambda e: e.tensor_tensor(out=out, in0=in_, in1=ex, op=ALU.pow), list(r) + ["cpow"], w, sync=sync)

        def DMA(out, in_, r, w):
            P.op("sp", lambda e: e.dma_start(out=out, in_=in_), r, w)

        P.force_sync = True
        DMA(ident[:], ident_d, [], ["ident"])
        DMA(cc[:], cc_d, [], ["cc"])
        CP("pool", identb[:], ident[:], ["ident"], ["identb"])
        MS("pool", onesm[:], 1.0 / 512.0, ["onesm"])
        MS("pool", cq[:], 0.25, ["cq"])
        MS("pool", cpow[:, 0:1], 0.5, ["cpow"])
        MS("pool", cpow[:, 1:2], -0.5, ["cpow"])
        MS("pool", V1pad[:], 0.0, ["V1pad0", "V1pad1"])
        MS("pool", Vc[:], 0.0, ["Vc"])
        ACT(cc[:], cc[:], AF.Silu, ["cc"], ["cc"])
        P.force_sync = False

        BIG = ["big.lo", "big.hi"]

        def load_w_in(l, ocs, groups=(0, 1, 2)):
            for g in groups:
                oc0 = ocs[4 * g]
                src = win_d[l].rearrange("(k p) n -> p k n", p=128)[:, :, oc0 * 128:(oc0 + 4) * 128]
                DMA(stage, src, [], BIG)
                for kc in range(8):
                    if kc % 3 == 2:
                        ACT(w_in_bf[:, kc, g * 512:(g + 1) * 512], stage[:, kc, :], AF.Copy, BIG, ["w_in_bf"])
                    else:
                        CP("dve", w_in_bf[:, kc, g * 512:(g + 1) * 512], stage[:, kc, :], BIG, ["w_in_bf"])

        def load_w_out(l):
            for g in range(2):
                src = wout_d[l].rearrange("(k p) n -> p k n", p=128)[:, :, g * 512:(g + 1) * 512]
                DMA(stage, src, [], BIG)
                for kc in range(8):
                    if kc % 3 == 2:
                        ACT(w_out_bf[:, kc, g * 512:(g + 1) * 512], stage[:, kc, :], AF.Copy, BIG, ["w_out_bf"])
                    else:
                        CP("dve", w_out_bf[:, kc, g * 512:(g + 1) * 512], stage[:, kc, :], BIG, ["w_out_bf"])

        def build_dgm(ccs):
            for i, c4 in enumerate(ccs):
                in0 = identb[:].unsqueeze(1).to_broadcast([128, 31, 128])
                in1 = PPH[:, c4:124:4].unsqueeze(2).to_broadcast([128, 31, 128])
                TT("dve", dgm[:, :, i, :], in0, in1, ALU.mult, ["identb", "PPH"], ["dgm"])

        def layer_setup(l):
            P.force_sync = True
            layer_setup_(l)
            P.force_sync = False

        def layer_setup_(l):
            DMA(PP[:], pp_d[l], [], ["PP"])
            pt, pn = nb()
            for hs in range(12):
                half = big[:, (hs % 2) * 2048:(hs % 2 + 1) * 2048].rearrange("p (k n) -> p k n", k=8)
                hname = BIG[hs % 2]
                src = wmod_d[l].rearrange("(k p) n -> p k n", p=128)[:, :, hs * 256:(hs + 1) * 256]
                P.op("sp", lambda e, half=half, src=src: e.dma_start(out=half, in_=src), [], [hname])
                for o in range(2):
                    oc = hs * 2 + o
                    for kc in range(8):
                        MM(pt[:, oc * 2:oc * 2 + 2], half[:, kc, o * 128:(o + 1) * 128], cc[:, kc * 2:kc * 2 + 2],
                           kc == 0, kc == 7, [hname, "cc"], [pn])
            CP("dve", modT[:], pt[:, 0:48], [pn], ["modT"])
            m3 = modT[:].rearrange("p (o j) -> p o j", j=2)
            for j in range(2):
                a_j = Apre[:].rearrange("p (k j) -> p k j", j=2)[:, :, j]
                b_j = Bpre[:].rearrange("p (k j) -> p k j", j=2)[:, :, j]
                TT("dve", a_j, m3[:, 8:16, j], PP[:, C_BSC:C_BSC + 8], ALU.add, ["modT", "PP"], ["Apre"])
                TS("dve", a_j, a_j, 1.0, None, ALU.add, None, ["Apre"], ["Apre"])
                TT("dve", a_j, a_j, PP[:, C_GPRE:C_GPRE + 8], ALU.mult, ["Apre", "PP"], ["Apre"])
                TT("dve", b_j, m3[:, 0:8, j], PP[:, C_BSH:C_BSH + 8], ALU.add, ["modT", "PP"], ["Bpre"])
                g_j = gT[:, j * 8:(j + 1) * 8]
                TT("dve", g_j, m3[:, 16:24, j], PP[:, C_BGT:C_BGT + 8], ALU.add, ["modT", "PP"], ["gT"])
                TT("dve", g_j, g_j, PP[:, C_GPOST:C_GPOST + 8], ALU.mult, ["gT", "PP"], ["gT"])
            pt, pn = nb()
            MM(pt[0:16, 0:128], gT[:, 0:16], ident[:], True, True, ["gT", "ident"], [pn])
            CP("dve", gTt[:], pt[0:16, 0:128], [pn], ["gTt"])
            DMA(g_d[l].rearrange("j (k p) -> (j k) p", p=128), gTt[:], ["gTt"], ["g_d"])
            DMA(big[:, 0:2048], wg_d[l], [], ["big.lo"])
            CP("dve", wg_bf[:].rearrange("p a b -> p (a b)"), big[:, 0:2048], ["big.lo"], ["wg_bf"])
            ACT(kap[:], PP[:, C_LAM:C_LAM + 8], AF.Exp, ["PP"], ["kap"], scale=-1.0)
            ACT(kap[:], kap[:], AF.Ln, ["kap"], ["kap"], bias=1.0)
            TS("dve", kaph[:], kap[:], -4.0, None, ALU.mult, None, ["kap"], ["kaph"])
            TS("dve", kap[:], kap[:], -8.0, None, ALU.mult, None, ["kap"], ["kap"])
            TS("dve", PPH[:, 0:124], PP[:, C_DWW:C_DWW + 124], 0.5, None, ALU.mult, None, ["PP"], ["PPH"])
            TS("dve", PPH[:, 124:140], PP[:, C_BR:C_BR + 16], 0.5, None, ALU.mult, None, ["PP"], ["PPH"])
            MS("pool", carry[:], 0.0, [f"carry{i}" for i in range(8)])

        src_rows_names = [[]]

        def load_x(src_rows, N, slot):
            nt = N // 128
            P.op("sp", lambda e: e.dma_start(out=xs[slot][:, 0:nt, :], in_=src_rows.rearrange("(t p) d -> p t d", p=128)),
                 src_rows_names[0], [f"xs{slot}"])

        def stats(N, slot, pb):
            nt = N // 128
            X = xs[slot]
            MS("dve", ss[pb][:], 0.0, [f"ss{pb}"], sync=True)
            for t in range(nt):
                ACT(junk, X[:, t, :], AF.Square, [f"xs{slot}"], JUNK + [f"ss{pb}"], accum_out=ss[pb][:, t:t + 1])
            TS("dve", rstd[pb][:, 0:nt], ss[pb][:, 0:nt], 1.0 / D, EPS, ALU.mult, ALU.add, [f"ss{pb}"], [f"rstd{pb}"], sync=True)
            POW(rstd[pb][:, 0:nt], rstd[pb][:, 0:nt], 1, nt, [f"rstd{pb}"], [f"rstd{pb}"], sync=True)
            for t in range(nt):
                TS("dve", Dg[pb][:, t, :], ident[:], rstd[pb][:, t:t + 1], None, ALU.mult, None,
                   ["ident", f"rstd{pb}"], [f"Dg{pb}_{t}"], sync=True)

        def transposes(N, j, slot, pb, store=None):
            nt = N // 128
            X = xs[slot]
            hxT = hxTb[0]
            for kc in range(8):
                pt, pn = nb()
                for t in range(nt):
                    MM(pt[:, t * 128:(t + 1) * 128], X[:, t, kc * 128:(kc + 1) * 128], Dg[pb][:, t, :], True, True,
                       [f"xs{slot}", f"Dg{pb}_{t}"], [pn])
                ACT(hxT[:, kc, 0:N], pt[:, 0:N], AF.Identity, [pn, "Apre", "Bpre"], [f"hxT0_{kc}"],
                    scale=Apre[:, kc * 2 + j:kc * 2 + j + 1], bias=Bpre[:, kc * 2 + j:kc * 2 + j + 1])
            if store is not None:
                dst, dname = store
                DMA(dst, hxT[:, :, 0:N], [f"hxT0_{kc}" for kc in range(8)], [dname])

        def inproj(widx, N, hb=0):
            pt, pn = nb()
            for kc in range(8):
                MM(pt[:, 0:N], w_in_bf[:, kc, widx * 128:(widx + 1) * 128], hxTb[hb][:, kc, 0:N], kc == 0, kc == 7,
                   ["w_in_bf", f"hxT{hb}_{kc}"], [pn])
            return pt, pn

        AVN = [f"av{c}" for c in range(4)]

        def a_branch_val(N):
            for c4 in range(4):
                pt, pn = inproj(c4, N)
                ACT(av[:, c4, 3:3 + N], pt[:, 0:N], AF.Identity, [pn], [f"av{c4}"])

        def lruA_all(d, N):
            base = 0 if d == 0 else 3
            wcol = lambda k, c4: PP[:, C_CAW + d * 16 + k * 4 + c4:C_CAW + d * 16 + k * 4 + c4 + 1]
            for c4 in range(4):
                bcol = PP[:, C_CAB + d * 4 + c4:C_CAB + d * 4 + c4 + 1]
                TS("pool", lvc[c4][:, 0:N], av[:, c4, base:base + N], wcol(0, c4), bcol, ALU.mult, ALU.add,
                   [f"av{c4}", "PP"], [f"lvc{c4}"])
            for c4 in range(4):
                vc = lvc[c4]
                for k in range(1, 4):
                    STT("dve", vc[:, 0:N], av[:, c4, base + k:base + k + N], wcol(k, c4), vc[:, 0:N], ALU.mult, ALU.add,
                        [f"av{c4}", "PP", f"lvc{c4}"], [f"lvc{c4}"])
            for c4 in range(4):
                ACT(lvb[c4][:, 0:N], lvc[c4][:, 0:N], AF.Copy, [f"lvc{c4}"], [f"lvb{c4}"])

        def lruB1(d, c4, N):
            si = c4 % NSET
            S = lt[si]
            nm = lambda k: f"l{k}{si}"
            A, B, C = S["A"], S["B"], S["C"]
            pr, prn = nb()
            MM(pr[:, 0:N], wg_bf[:, (d * 2 + 0) * 4 + c4, :], lvb[c4][:, 0:N], True, True, ["wg_bf", f"lvb{c4}"], [prn])
            pi, pin = nb()
            MM(pi[:, 0:N], wg_bf[:, (d * 2 + 1) * 4 + c4, :], lvb[c4][:, 0:N], True, True, ["wg_bf", f"lvb{c4}"], [pin])
            kc_ = d * 4 + c4
            hb_r = PPH[:, 124 + kc_:125 + kc_]
            hb_i = PPH[:, 132 + kc_:133 + kc_]
            ACT(A[:, 0:N], pr[:, 0:N], AF.Tanh, [prn, "PPH"], [nm("A")], scale=0.5, bias=hb_r)
            ACT(B[:, 0:N], pi[:, 0:N], AF.Tanh, [pin, "PPH"], [nm("B")], scale=0.5, bias=hb_i)
            ACT(C[:, 0:N], A[:, 0:N], AF.Exp, [nm("A"), "kap"], [nm("C")], scale=kap[:, kc_:kc_ + 1], bias=kap[:, kc_:kc_ + 1])
            ACT(A[:, 0:N], A[:, 0:N], AF.Exp, [nm("A"), "kaph"], [nm("A")], scale=kaph[:, kc_:kc_ + 1], bias=kaph[:, kc_:kc_ + 1])

        def lruBs(d, c4, N):
            si = c4 % NSET
            C = lt[si]["C"]
            ACT(C[:, 0:N], C[:, 0:N], AF.Sqrt, [f"lC{si}", "cq"], [f"lC{si}"], scale=-0.25, bias=cq[:, 0:1])

        def lruB2(d, c4, N):
            si = c4 % NSET
            S = lt[si]
            nm = lambda k: f"l{k}{si}"
            vc, A, B, C, H = lvc[c4], S["A"], S["B"], S["C"], S["H"]
            kc_ = d * 4 + c4
            STT("dve", B[:, 0:N], B[:, 0:N], 1.0, vc[:, 0:N], ALU.add, ALU.mult, [nm("B"), f"lvc{c4}"], [nm("B")])
            TT("dve", B[:, 0:N], B[:, 0:N], C[:, 0:N], ALU.mult, [nm("B"), nm("C")], [nm("B")])
            cin = carry[:, kc_:kc_ + 1]
            if d == 0:
                P.op("dve", lambda e: e.tensor_tensor_scan(out=H[:, 0:N], data0=A[:, 0:N], data1=B[:, 0:N], initial=cin,
                                                           op0=ALU.mult, op1=ALU.add),
                     [nm("A"), nm("B"), f"carry{kc_}"], [nm("H")])
                CP("pool", cin, H[:, N - 1:N], [nm("H")], [f"carry{kc_}"], sync=True)
            else:
                P.op("dve", lambda e: e.tensor_tensor_scan(out=H[:, 0:N][:, ::-1], data0=A[:, 0:N][:, ::-1],
                                                           data1=B[:, 0:N][:, ::-1], initial=cin,
                                                           op0=ALU.mult, op1=ALU.add),
                     [nm("A"), nm("B"), f"carry{kc_}"], [nm("H")], sync=True)
                CP("pool", cin, H[:, 0:1], [nm("H")], [f"carry{kc_}"], sync=True)
            return H, nm("H")

        def lru_all(d, N, consume, extra_sqrt=lambda: None, inter=lambda k: None, hook=lambda tag: None):
            for half in range(2):
                c4s = (2 * half, 2 * half + 1)
                for c4 in c4s:
                    lruB1(d, c4, N)
                hook(f"b1_{half}")
                for c4 in c4s:
                    lruBs(d, c4, N)
                if half == 0:
                    extra_sqrt()
                for c4 in c4s:
                    inter(c4)
                    H, hn = lruB2(d, c4, N)
                    consume(c4, H, hn)
                hook(f"done_{half}")

        vt_ctr = [0]

        def pass1_chunk(l, c, nxt, snx):
            isctx = c < 0
            N = TC if isctx else 512
            full = (not isctx) or (l == 0)
            a_branch_val(N)
            if isctx:
                DMA(avc_d.rearrange("(c p) t -> p c t", p=128), av[:, :, 3:3 + N], AVN, ["avd-1"])
            else:
                DMA(av_d.rearrange("(c p) t -> p c t", p=128)[:, :, c * 512:(c + 1) * 512], av[:, :, 3:3 + N], AVN, [f"avd{c}"])
            lruA_all(0, N)
            snx()
            if full:
                for c4 in range(4):
                    pv, pvn = inproj(4 + c4, N)
                    pg, pgn = inproj(8 + c4, N)
                    ACT(sg[:, 0:N], pg[:, 0:N], AF.Tanh, [pgn], ["sg"], scale=0.5)
                    if isctx:
                        STT("dve", Vc[:, c4, 15:15 + N], sg[:, 0:N], 1.0, pv[:, 0:N], ALU.add, ALU.mult, [pvn, "sg"], ["Vc"])
                    elif c4 < 2:
                        STT("dve", V1pad[:, c4, :, 15:79], sg[:, 0:N].rearrange("p (r w) -> p r w", w=64), 1.0,
                            pv[:, 0:N].rearrange("p (r w) -> p r w", w=64), ALU.add, ALU.mult, [pvn, "sg"], [f"V1pad{c4}"])
                    else:
                        vi = vt_ctr[0] % 2
                        vt_ctr[0] += 1
                        STT("dve", vtmp[vi][:], sg[:, 0:N], 1.0, pv[:, 0:N], ALU.add, ALU.mult, [pvn, "sg"], ["vtmp0"])
                        DMA(v2_d[(c4 - 2) * 128:(c4 - 1) * 128, c * 512:(c + 1) * 512], vtmp[vi][:], ["vtmp0"], [f"v2d{c}_{c4}"])
                for c4 in range(2):
                    pt, pn = nb()
                    for k in range(31):
                        if isctx:
                            MM(pt[:, 0:N], dgm[:, k, c4, :], Vc[:, c4, k:k + N], k == 0, k == 30, ["dgm", "Vc"], [pn])
                        else:
                            MM(pt[:].rearrange("p (r w) -> p r w", w=64), dgm[:, k, c4, :], V1pad[:, c4, :, k:k + 64],
                               k == 0, k == 30, ["dgm", f"V1pad{c4}"], [pn])
                    bcol = PP[:, C_DWB + c4:C_DWB + c4 + 1]
                    if isctx:
                        ACT(ctx_y[:, c4, :], pt[:, 0:N], AF.Identity, [pn, "PP"], ["ctx_y"], bias=bcol)
                    else:
                        ACT(y1buf[c4][:], pt[:], AF.Identity, [pn, "PP"], [("ysq", "mean")[c4]], bias=bcol)
                        DMA(y1_d[c4 * 128:(c4 + 1) * 128, c * 512:(c + 1) * 512], y1buf[c4][:], [("ysq", "mean")[c4]], [f"y1d{c}_{c4}"])
            def consume1(c4, H, hn):
                if isctx:
                    if l == 0:
                        CP("pool", ctx_recf[:, c4, :], H[:, 0:N], [hn], ["ctx_recf"])
                else:
                    DMA(recf_d[c4 * 128:(c4 + 1) * 128, c * 512:(c + 1) * 512], H[:, 0:N], [hn], [f"recfd{c}_{c4}"])

            lru_all(0, N, consume1)
            CP("pool", av[:, :, 0:3], av[:, :, N:N + 3], AVN, AVN, sync=True)
            nxt()

        def p2_conv(i, N, isctx):
            c4 = 2 + i
            pt, pn = nb()
            for k in range(31):
                if isctx:
                    MM(pt[:, 0:N], dgm[:, k, i, :], Vc[:, c4, k:k + N], k == 0, k == 30, ["dgm", "Vc"], [pn])
                else:
                    MM(pt[:, 0:N], dgm[:, k, i, :], V2win[:, i, k * 64:k * 64 + 512], k == 0, k == 30,
                       ["dgm", "V2win"], [pn])
            ACT(ybuf[:, c4, 0:N], pt[:, 0:N], AF.Identity, [pn, "PP"], ["big.hi"], bias=PP[:, C_DWB + c4:C_DWB + c4 + 1])

        def p2_stats(N, isctx):
            if isctx:
                CP("pool", ybuf[:, 0:2, 0:N], ctx_y[:], ["ctx_y"], ["big.hi"])
            pm, pmn = nb()
            for c4 in range(4):
                MM(pm[:, 0:N], onesm[:], ybuf[:, c4, 0:N], c4 == 0, c4 == 3, ["onesm", "big.hi"], [pmn])
            pq, pqn = nb()
            for c4 in range(4):
                yq, yqn = ((ysq, "ysq"), (ysqB, "ysqB"))[c4 % 2]
                ACT(yq[:, 0:N], ybuf[:, c4, 0:N], AF.Square, ["big.hi"], [yqn])
                MM(pq[:, 0:N], onesm[:], yq[:, 0:N], c4 == 0, c4 == 3, ["onesm", yqn], [pqn])
            CP("dve", mean[:, 0:N], pm[:, 0:N], [pmn], ["mean"])
            TT("dve", rs_ln[:, 0:N], mean[:, 0:N], mean[:, 0:N], ALU.mult, ["mean"], ["rs_ln"])
            TT("dve", rs_ln[:, 0:N], pq[:, 0:N], rs_ln[:, 0:N], ALU.subtract, [pqn, "rs_ln"], ["rs_ln"])
            TS("dve", rs_ln[:, 0:N], rs_ln[:, 0:N], EPS, None, ALU.add, None, ["rs_ln"], ["rs_ln"])

        def p2_ln_sqrt(N):
            ACT(rs_ln[:, 0:N], rs_ln[:, 0:N], AF.Sqrt, ["rs_ln"], ["rs_ln"])
            P.op("dve", lambda e: e.reciprocal(out=rs_ln[:, 0:N], in_=rs_ln[:, 0:N]), ["rs_ln"], ["rs_ln"])

        def p2_ln_iter(c4, N, hb, mb):
            b = c4 % 2
            p_, pn1 = ((ysq, "ysq"), (ysqB, "ysqB"))[b]
            TT("pool", zt[:, 0:N], ybuf[:, c4, 0:N], mean[:, 0:N], ALU.subtract, ["big.hi", "mean"], ["zt"])
            TT("dve", zt[:, 0:N], zt[:, 0:N], rs_ln[:, 0:N], ALU.mult, ["zt", "rs_ln"], ["zt"])
            ACT(zt[:, 0:N], zt[:, 0:N], AF.Identity, ["zt", "PP"], ["zt"],
                scale=PP[:, C_LNG + c4:C_LNG + c4 + 1], bias=PP[:, C_LNB + c4:C_LNB + c4 + 1])
            ACT(p_[:, 0:N], zt[:, 0:N], AF.Tanh, ["zt"], [pn1], scale=0.5)
            pt, pn = inproj(8 + c4, N, hb)
            ACT(sbg[:, 0:N], pt[:, 0:N], AF.Tanh, [pn], ["sbg"], scale=0.5)
            STT("dve", p_[:, 0:N], p_[:, 0:N], 1.0, zt[:, 0:N], ALU.add, ALU.mult, [pn1, "zt"], [pn1])
            STT("dve", sbg[:, 0:N], sbg[:, 0:N], 1.0, pt[:, 0:N], ALU.add, ALU.mult, [pn, "sbg"], ["sbg"])
            STT("dve", mixB[mb][:, c4, 0:N], p_[:, 0:N], 0.25, sbg[:, 0:N], ALU.mult, ALU.mult, [pn1, "sbg"], [f"mixB{mb}_{c4}"])

        def p2_lru(l, cn, k, extra, inter, hook=lambda tag: None):
            isctx = cn < 0
            N = TC if isctx else 512
            full = (not isctx) or (l == 0)
            hb = k % 2
            par = k % 3

            def consume2(c4, H, hn):
                if not full:
                    return
                pt, pn = inproj(4 + c4, N, hb)
                ACT(sag[:, 0:N], pt[:, 0:N], AF.Tanh, [pn], ["sg"], scale=0.5)
                STT("dve", sag[:, 0:N], sag[:, 0:N], 1.0, pt[:, 0:N], ALU.add, ALU.mult, [pn, "sg"], ["sg"])
                rf = ctx_recf[:, c4, :] if isctx else recf[:, c4, :]
                rfn = "ctx_recf" if isctx else "big.lo"
                TT("pool", H[:, 0:N], H[:, 0:N], rf, ALU.add, [hn, rfn], [hn])
                STT("dve", mixA[par][:, c4, 0:N], H[:, 0:N], 0.5, sag[:, 0:N], ALU.mult, ALU.mult, [hn, "sg"], [f"mixA{par}_{c4}"])

            lru_all(1, N, consume2, extra, inter, hook)

        def p2_out_tile(k, t, dst_rows, dst_name):
            X = xs[0]
            pa, pb_ = k % 3, k % 2
            halves = []
            MS("dve", ss2[:], 0.0, ["ss2"], sync=True)
            for h in range(2):
                pt, pn = nb()
                for kc in range(8):
                    lhs = mixA[pa][:, kc, t * 128:(t + 1) * 128] if kc < 4 else mixB[pb_][:, kc - 4, t * 128:(t + 1) * 128]
                    lname = f"mixA{pa}_{kc}" if kc < 4 else f"mixB{pb_}_{kc - 4}"
                    MM(pt[:], lhs, w_out_bf[:, kc, h * 512:(h + 1) * 512], kc == 0, kc == 7, [lname, "w_out_bf"], [pn])
                ACT(vtmp[0][:], pt[:], AF.Square, [pn], ["vtmp0", "ss2"], accum_out=ss2[:, h:h + 1])
                halves.append((pt, pn))
            TT("dve", rstd2[:], ss2[:, 0:1], ss2[:, 1:2], ALU.add, ["ss2"], ["rstd2"], sync=True)
            TS("dve", rstd2[:], rstd2[:], 1.0 / D, EPS, ALU.mult, ALU.add, ["rstd2"], ["rstd2"], sync=True)
            POW(rstd2[:], rstd2[:], 1, 1, ["rstd2"], ["rstd2"], sync=True)
            for h, (pt, pn) in enumerate(halves):
                STT("dve", ptmp[h][:], pt[:], rstd2[:, 0:1], G[:, h * 512:(h + 1) * 512], ALU.mult, ALU.mult,
                    [pn, "rstd2", "G"], [("zt", "sbg")[h]])
                TT("pool", X[:, t, h * 512:(h + 1) * 512], ptmp[h][:], X[:, t, h * 512:(h + 1) * 512], ALU.add,
                   [("zt", "sbg")[h], f"xr{t}"], [f"xr{t}"])
            deferred_stores.append(lambda: DMA(dst_rows[t * 128:(t + 1) * 128, :], X[:, t, :], [f"xr{t}"], [f"{dst_name}_{t}"]))

        deferred_stores = []

        XR = [f"xr{t}" for t in range(4)]
        for l in range(L):
            xin = x_d if l == 0 else x1_d
            xout = x1_d if l == 0 else out_d
            cin_d = ctx_d if l == 0 else xc1_d
            xname = (lambda c: []) if l == 0 else (lambda c: [f"x1_{c}_{t}" for t in range(4)])
            cname = [] if l == 0 else ["xc1_0", "xc1_1"]
            oname = (lambda c: f"x1_{c}") if l == 0 else (lambda c: f"out_{c}")

            def rows(c):
                return cin_d if c < 0 else xin[c * 512:(c + 1) * 512, :]

            def rnames(c):
                return cname if c < 0 else xname(c)

            def ld(c):
                src_rows_names[0] = rnames(c)
                load_x(rows(c), TC if c < 0 else 512, 0)

            def NJ(c):
                return (TC, 1) if c < 0 else (512, 0)

            def hx_store(c):
                if c < 0:
                    return (hxc_d.rearrange("(k p) t -> p k t", p=128), "hxd-1")
                return (hx_d.rearrange("(k p) t -> p k t", p=128)[:, :, c * 512:(c + 1) * 512], f"hxd{c}")

            layer_setup(l)
            load_w_in(l, [0, 1, 2, 3, 8, 9, 10, 11, 12, 13, 14, 15])
            build_dgm([0, 1])
            MS("pool", av[:], 0.0, AVN)
            P.op("pool", lambda e: e.memset(dummy[:, 0:1], 0.0), XR + MA2, ["xs0"] + DGN)
            seq = [-1] + list(range(NCH))
            ld(seq[0])
            n0, j0 = NJ(seq[0])
            stats(n0, 0, 0)
            transposes(n0, j0, 0, 0, hx_store(seq[0]))
            for i, c in enumerate(seq):
                if i + 1 < len(seq):
                    c1 = seq[i + 1]
                    n1, j1 = NJ(c1)
                    p1 = (i + 1) % 2
                    ld(c1)
                    snx = (lambda n1=n1, p1=p1: stats(n1, 0, p1))
                    nxt = (lambda n1=n1, j1=j1, p1=p1, c1=c1: transposes(n1, j1, 0, p1, hx_store(c1)))
                else:
                    snx = (lambda: None)
                    nxt = (lambda: None)
                pass1_chunk(l, c, nxt, snx)
                if c < 0:
                    MS("pool", av[:], 0.0, AVN)
                if i == 5:
                    load_w_out(l)
            load_w_in(l, [0, 1, 2, 3, 4, 5, 6, 7, 16, 17, 18, 19], groups=(1, 2))
            build_dgm([2, 3])
            MS("pool", av[:], 0.0, AVN)
            P.op("pool", lambda e: e.memset(dummy[:, 1:2], 0.0), ["xs0"] + DGN, XR + MA2)
            if l == 0:
                DMA(G[:], g_d[l, 1].partition_broadcast(128), ["g_d"], ["G"])
            else:
                DMA(G[:], g_d[l, 0].partition_broadcast(128), ["g_d"], ["G"])

            def loads_A(c, hb):
                n = TC if c < 0 else 512
                full_c = (c >= 0) or (l == 0)
                if c >= 0:
                    DMA(recf, recf_d.rearrange("(c p) t -> p c t", p=128)[:, :, c * 512:(c + 1) * 512], [f"recfd{c}_{q}" for q in range(4)], ["big.lo"])
                if full_c:
                    src = hxc_d.rearrange("(k p) t -> p k t", p=128) if c < 0 else \
                        hx_d.rearrange("(k p) t -> p k t", p=128)[:, :, c * 512:(c + 1) * 512]
                    DMA(hxTb[hb][:, :, 0:n], src, [f"hxd{c}"], [f"hxT{hb}_{kc}" for kc in range(8)])
                if c < 0:
                    DMA(av[:, :, 3:3 + n], avc_d.rearrange("(c p) t -> p c t", p=128), ["avd-1"], AVN)
                elif c == NCH - 1:
                    MS("pool", av[:, :, 3 + n:6 + n], 0.0, AVN)
                    DMA(av[:, :, 3:3 + n], av_d.rearrange("(c p) t -> p c t", p=128)[:, :, c * 512:(c + 1) * 512], [f"avd{c}"], AVN)
                else:
                    DMA(av[:, :, 3:6 + n], av_d.rearrange("(c p) t -> p c t", p=128)[:, :, c * 512:(c + 1) * 512 + 3],
                        [f"avd{c}", f"avd{c + 1}"], AVN)

            def loads_B(c):
                DMA(ybuf[:, 0:2, :], y1_d.rearrange("(c p) t -> p c t", p=128)[:, :, c * 512:(c + 1) * 512], [f"y1d{c}_{q}" for q in range(2)], ["big.hi"])
                r0 = 8 * c - 15
                lo, hi = max(0, r0), min(128, r0 + 38)
                if lo > r0 or hi < r0 + 38:
                    MS("pool", V2win[:], 0.0, ["V2win"])
                DMA(V2win[:, :, (lo - r0) * 64:(hi - r0) * 64],
                    v2_d.rearrange("(c p) t -> p c t", p=128)[:, :, lo * 64:hi * 64],
                    [f"v2d{q}_{r}" for q in range(max(0, c - 2), min(NCH, c + 3)) for r in (2, 3)], ["V2win"])

            seq = [-1] + list(range(NCH - 1, -1, -1))
            nseq = len(seq)
            isfull = lambda c: (c >= 0) or (l == 0)
            loads_A(seq[0], 0)
            lruA_all(1, TC)
            p2_lru(l, seq[0], 0, lambda: None, lambda k: None)
            loads_A(seq[1], 1)
            lruA_all(1, 512)
            for i in range(nseq + 1):
                c = seq[i] if i < nseq else None
                cn = seq[i + 1] if i + 1 < nseq else None
                cnn = seq[i + 2] if i + 2 < nseq else None
                cp = seq[i - 1] if i >= 1 else None
                doB = c is not None and isfull(c)
                doO = cp is not None and isfull(cp)
                Nc = TC if (c is not None and c < 0) else 512
                Np = TC if (cp is not None and cp < 0) else 512
                hb = i % 2
                if doO:
                    for t in range(Np // 128):
                        DMA(xs[0][:, t, :], rows(cp)[t * 128:(t + 1) * 128, :], rnames(cp), [f"xr{t}"])
                    orow = xc1_d if cp < 0 else xout[cp * 512:(cp + 1) * 512, :]
                    onm = "xc1" if cp < 0 else oname(cp)

                def hook(tag, c=c, cp=cp, doB=doB, doO=doO, Nc=Nc, Np=Np, i=i):
                    order = ["b1_0", "done_0", "b1_1", "done_1"]
                    t = order.index(tag)
                    if doB and t < 2:
                        p2_conv(t, Nc, c < 0)
                    if doO and t < Np // 128:
                        p2_out_tile(i - 1, t, orow, onm)
                    if doB and t == 2:
                        p2_stats(Nc, c < 0)

                if cn is not None:
                    p2_lru(l, cn, i + 1, lambda: None, lambda k: None, hook)
                else:
                    for tag in ("b1_0", "done_0", "b1_1", "done_1"):
                        hook(tag)
                if doO and cp < 0:
                    DMA(G[:], g_d[l, 0].partition_broadcast(128), ["g_d"], ["G"])
                if doB:
                    p2_ln_sqrt(Nc)
                    for k in range(4):
                        p2_ln_iter(k, Nc, hb, i % 2)
                if cnn is not None:
                    loads_A(cnn, (i + 2) % 2)
                    lruA_all(1, 512)
                if cn is not None and cn >= 0:
                    loads_B(cn)
                for st_ in deferred_stores:
                    st_()
                deferred_stores.clear()

        P.emit()
    return nc


def _pack_inputs(inp, b):
    f = lambda a: np.ascontiguousarray(np.asarray(a, dtype=np.float32))
    fm = lambda v, n: v.reshape(n, 128).T
    cc = np.zeros((128, 16), np.float32)
    cc[:, 0::2] = fm(f(inp["c"])[b], 8)
    cc[:, 1::2] = fm(f(inp["c_ctx"]), 8)
    pp = np.zeros((L, 128, NPP), np.float32)
    wg = np.zeros((L, 128, 16, 128), np.float32)
    for l in range(L):
        bm = f(inp["b_mod"])[l]
        pp[l, :, C_GPRE:C_GPRE + 8] = fm(f(inp["g_pre"])[l], 8)
        pp[l, :, C_BSH:C_BSH + 8] = fm(bm[0:D], 8)
        pp[l, :, C_BSC:C_BSC + 8] = fm(bm[D:2 * D], 8)
        pp[l, :, C_BGT:C_BGT + 8] = fm(bm[2 * D:3 * D], 8)
        pp[l, :, C_GPOST:C_GPOST + 8] = fm(f(inp["g_post"])[l], 8)
        for d in range(2):
            for k in range(4):
                pp[l, :, C_CAW + d * 16 + k * 4:C_CAW + d * 16 + k * 4 + 4] = fm(f(inp["conv_a_w"])[l, d, k], 4)
            pp[l, :, C_CAB + d * 4:C_CAB + d * 4 + 4] = fm(f(inp["conv_a_b"])[l, d], 4)
            pp[l, :, C_BR + d * 4:C_BR + d * 4 + 4] = fm(f(inp["b_rgate"])[l, d], 4)
            pp[l, :, C_BI + d * 4:C_BI + d * 4 + 4] = fm(f(inp["b_igate"])[l, d], 4)
            pp[l, :, C_LAM + d * 4:C_LAM + d * 4 + 4] = fm(f(inp["lru_lambda"])[l, d], 4)
            for gi, key in enumerate(("w_rgate", "w_igate")):
                w = f(inp[key])[l, d]
                for c4 in range(4):
                    for hh in range(2):
                        wg[l, hh * 64:(hh + 1) * 64, (d * 2 + gi) * 4 + c4, hh * 64:(hh + 1) * 64] = w[c4 * 2 + hh]
        for k in range(31):
            pp[l, :, C_DWW + k * 4:C_DWW + k * 4 + 4] = fm(f(inp["dw_w"])[l, k], 4)
        pp[l, :, C_DWB:C_DWB + 4] = fm(f(inp["dw_b"])[l], 4)
        pp[l, :, C_LNG:C_LNG + 4] = fm(f(inp["ln_g"])[l], 4)
        pp[l, :, C_LNB:C_LNB + 4] = fm(f(inp["ln_b"])[l], 4)
    return {
        "x": f(inp["x"])[b], "ctx": f(inp["ctx"])[b], "cc": cc, "pp": pp,
        "w_mod": f(inp["w_mod"]), "w_in": f(inp["w_in"]), "w_out": f(inp["w_out"]),
        "wg": wg.reshape(L, 128, 16 * 128), "ident": np.eye(128, dtype=np.float32),
    }


_NC_CACHE = {}


def kernel(**inputs):
    if "nc" not in _NC_CACHE:
        _NC_CACHE["nc"] = build_program()
    nc = _NC_CACHE["nc"]
    maps = [_pack_inputs(inputs, b) for b in range(2)]
    in_maps = [maps[r % 2] for r in range(8)]
    res = run_bass_kernel_spmd(nc, in_maps, core_ids=list(range(8)))
    out = np.stack([np.asarray(res.results[b]["out"], dtype=np.float32) for b in range(2)], axis=0)
    if DEBUG_OUT:
        kernel.debug = [{k: np.asarray(res.results[b][k]) for k in ("x1", "xc1")} for b in range(2)]
    return out
```
